# Optimizing a Trainium2 kernel written in Bass

```python
import math
import jax, jax.numpy as jnp
from jax import lax
import numpy as np

D_MODEL = 1024
BATCH = 4
SEQ = 4096
DEPTH = 4

N_MIXERS = 3
N_A = (DEPTH + 2) // 3
N_B = (DEPTH + 1) // 3
N_C = DEPTH // 3

PLE_DIM = 256
D_FF = 4 * D_MODEL
DEEP_ALPHA = (2.0 * DEPTH) ** 0.25
DEEP_BETA = (8.0 * DEPTH) ** -0.25
LN_EPS = 1e-5

REL_BUCKETS = 32
REL_MAX_DIST = 128

DA_HEADS = D_MODEL // 128
DA_HEAD_DIM = 64
DA_V_DIM = 2 * DA_HEAD_DIM
DA_EPS = 1e-5
Q_BLOCK = 128

GLA_HEADS = 4
GLA_DK = D_MODEL // 2 // GLA_HEADS
GLA_DV = D_MODEL // GLA_HEADS
GLA_GATE_RANK = 16
GLA_TAU = 16.0
GLA_CHUNK = 64
GLA_EPS = 1e-5

RW_HEAD = 64
RW_HEADS = D_MODEL // RW_HEAD
RW_DECAY_LORA = 64
RW_A_LORA = 64
RW_GATE_LORA = 128
RW_GN_EPS = 64e-5

kernel_name = 'hybrid_diffattn_gla_rwkv7_trunk'


def layer_norm(x, g, b, eps):
    xf = x.astype(jnp.float32)
    mu = jnp.mean(xf, -1, keepdims=True)
    var = jnp.mean(jnp.square(xf - mu), -1, keepdims=True)
    return ((xf - mu) * lax.rsqrt(var + eps)).astype(x.dtype) * g + b


def rms_norm(x, g, eps):
    xf = x.astype(jnp.float32)
    return (xf * lax.rsqrt(jnp.mean(xf * xf, -1, keepdims=True) + eps)).astype(x.dtype) * g


def t5_bucket(rel):
    n = jnp.maximum(rel, 0)
    max_exact = REL_BUCKETS // 2
    nf = jnp.maximum(n, 1).astype(jnp.float32)
    large = max_exact + (jnp.log(nf / max_exact) / math.log(REL_MAX_DIST / max_exact)
                         * (REL_BUCKETS - max_exact)).astype(jnp.int32)
    large = jnp.minimum(large, REL_BUCKETS - 1)
    return jnp.where(n < max_exact, n, large)


def diff_attention(x, w_qkv, w_o, lam_q1, lam_k1, lam_q2, lam_k2, subln_g, rel_bias, lam_init):
    B, S, _ = x.shape
    H, d = DA_HEADS, DA_HEAD_DIM
    q, k, v = jnp.split(x @ w_qkv, 3, axis=-1)
    q = q.reshape(B, S, H, 2, d) * (d ** -0.5)
    k = k.reshape(B, S, H, 2, d)
    v = v.reshape(B, S, H, DA_V_DIM)
    lam = (jnp.exp(jnp.sum(lam_q1 * lam_k1).astype(jnp.float32))
           - jnp.exp(jnp.sum(lam_q2 * lam_k2).astype(jnp.float32)) + lam_init)
    n_blocks = S // Q_BLOCK
    q_blocks = jnp.moveaxis(q.reshape(B, n_blocks, Q_BLOCK, H, 2, d), 1, 0)
    k_pos = jnp.arange(S)

    def attend_block(args):
        q_blk, blk = args
        q_pos = blk * Q_BLOCK + jnp.arange(Q_BLOCK)
        rel = q_pos[:, None] - k_pos[None, :]
        bias = jnp.moveaxis(rel_bias[t5_bucket(rel)], -1, 0).astype(jnp.float32)
        logits = jnp.einsum('bqhmd,bkhmd->bhmqk', q_blk, k).astype(jnp.float32) + bias[None, :, None]
        logits = jnp.where(rel >= 0, logits, -jnp.inf)
        probs = jax.nn.softmax(logits, axis=-1)
        attn = probs[:, :, 0] - lam * probs[:, :, 1]
        return jnp.einsum('bhqk,bkhe->bqhe', attn.astype(v.dtype), v)

    o = lax.map(attend_block, (q_blocks, jnp.arange(n_blocks)))
    o = jnp.moveaxis(o, 0, 1).reshape(B, S, H, DA_V_DIM)
    o = rms_norm(o, subln_g, DA_EPS) * (1.0 - lam_init)
    return o.reshape(B, S, H * DA_V_DIM) @ w_o


def gla(x, w_in, w_a1, w_a2, b_a, norm_g, w_o):
    B, S, _ = x.shape
    H, dk, dv, C = GLA_HEADS, GLA_DK, GLA_DV, GLA_CHUNK
    q, k, v, r = jnp.split(x @ w_in, [H * dk, 2 * H * dk, 2 * H * dk + H * dv], axis=-1)
    log_a = jax.nn.log_sigmoid((x @ w_a1) @ w_a2 + b_a) / GLA_TAU
    nC = S // C

    def to_chunks(t, e):
        return jnp.transpose(t.reshape(B, nC, C, H, e), (0, 3, 1, 2, 4)).astype(jnp.float32)

    q = to_chunks(q, dk) * (dk ** -0.5)
    k = to_chunks(k, dk)
    v = to_chunks(v, dv)
    b = jnp.cumsum(to_chunks(log_a, dk), axis=3)
    b_last = b[:, :, :, -1:]
    q_dec = q * jnp.exp(b)
    att = jnp.einsum('bhnid,bhnjd->bhnij', q_dec, k * jnp.exp(-b))
    att = jnp.where(jnp.tril(jnp.ones((C, C), bool)), att, 0.0)
    o_intra = jnp.einsum('bhnij,bhnje->bhnie', att, v)
    kv_chunk = jnp.einsum('bhnjd,bhnje->bhnde', k * jnp.exp(b_last - b), v)
    decay_chunk = jnp.exp(b_last[:, :, :, 0])

    def step(state, inp):
        kv_c, dec_c = inp
        return state * dec_c[..., None] + kv_c, state

    _, states_prev = lax.scan(step, jnp.zeros((B, H, dk, dv), jnp.float32),
                              (jnp.moveaxis(kv_chunk, 2, 0), jnp.moveaxis(decay_chunk, 2, 0)))
    states_prev = jnp.moveaxis(states_prev, 0, 2)
    o = o_intra + jnp.einsum('bhnid,bhnde->bhnie', q_dec, states_prev)
    o = jnp.transpose(o, (0, 2, 3, 1, 4)).reshape(B, S, H, dv)
    o = rms_norm(o, norm_g, GLA_EPS).astype(x.dtype) * jax.nn.silu(r).reshape(B, S, H, dv)
    return o.reshape(B, S, H * dv) @ w_o


def rwkv7_time_mix(x, mix, w_rkv, w0, w1, w2, a0, a1, a2, g1, g2, k_k, k_a, r_k, ln_g, ln_b, w_o):
    B, S, D = x.shape
    H, N = RW_HEADS, RW_HEAD
    xx = jnp.pad(x, ((0, 0), (1, 0), (0, 0)))[:, :-1] - x
    xm = x[:, :, None, :] + xx[:, :, None, :] * mix
    rkv = jnp.einsum('bsgd,gde->bsge', xm[:, :, :3], w_rkv)
    r, k, v = rkv[:, :, 0], rkv[:, :, 1], rkv[:, :, 2]
    xw, xa, xg = xm[:, :, 3], xm[:, :, 4], xm[:, :, 5]
    w_log = -jax.nn.softplus(-(w0 + jnp.tanh(xw @ w1) @ w2)) - 0.5
    decay = jnp.exp(-jnp.exp(w_log.astype(jnp.float32)))
    a = jax.nn.sigmoid(a0 + (xa @ a1) @ a2)
    g = jax.nn.sigmoid(xg @ g1) @ g2
    kk = (k * k_k).reshape(B, S, H, N).astype(jnp.float32)
    kk = kk * lax.rsqrt(jnp.maximum(jnp.sum(kk * kk, -1, keepdims=True), 1e-24))
    k = k * (1.0 + (a - 1.0) * k_a)

    def heads(t):
        return t.reshape(B, S, H, N).astype(jnp.float32)

    r_h, k_h, v_h, a_h, w_h = heads(r), heads(k), heads(v), heads(a), heads(decay)

    def step(state, inp):
        r_t, w_t, k_t, v_t, kk_t, a_t = inp
        sa = jnp.einsum('bhvk,bhk->bhv', state, -kk_t)
        state = (state * w_t[:, :, None, :] + sa[..., None] * (kk_t * a_t)[:, :, None, :]
                 + v_t[..., None] * k_t[:, :, None, :])
        return state, jnp.einsum('bhvk,bhk->bhv', state, r_t)

    seq_inputs = (jnp.moveaxis(r_h, 1, 0), jnp.moveaxis(w_h, 1, 0), jnp.moveaxis(k_h, 1, 0),
                  jnp.moveaxis(v_h, 1, 0), jnp.moveaxis(kk, 1, 0), jnp.moveaxis(a_h, 1, 0))
    _, y = lax.scan(step, jnp.zeros((B, H, N, N), jnp.float32), seq_inputs)
    y = jnp.moveaxis(y, 0, 1)
    y = layer_norm(y, ln_g.reshape(H, N), ln_b.reshape(H, N), RW_GN_EPS)
    y = y + jnp.sum(r_h * k_h * r_k, -1, keepdims=True) * v_h
    return (y.reshape(B, S, D).astype(x.dtype) * g) @ w_o


def sq_relu_mlp(x, w_up, w_down):
    return jnp.square(jax.nn.relu(x @ w_up)) @ w_down


def setup_inputs(seed: int = 0) -> dict:
    key = jax.random.key(seed)
    ks = iter(jax.random.split(key, 64))
    D = D_MODEL

    def nrm(shape, scale):
        return jax.random.normal(next(ks), shape, jnp.float32) * scale

    x = nrm((BATCH, SEQ, D), 1.0)
    p = nrm((DEPTH, BATCH, SEQ, PLE_DIM), 1.0)
    rel_bias = nrm((REL_BUCKETS, DA_HEADS), 0.5)
    ln1_g = 1.0 + nrm((DEPTH, D), 0.02)
    ln1_b = nrm((DEPTH, D), 0.02)
    ln2_g = 1.0 + nrm((DEPTH, D), 0.02)
    ln2_b = nrm((DEPTH, D), 0.02)
    mlp_up = nrm((DEPTH, D, D_FF), D ** -0.5)
    mlp_down = nrm((DEPTH, D_FF, D), D_FF ** -0.5 * DEEP_BETA)
    ple_proj = nrm((DEPTH, PLE_DIM, D), PLE_DIM ** -0.5)
    ple_gate = nrm((DEPTH, D, D), D ** -0.5)
    da_w_qkv = jnp.concatenate([nrm((N_A, D, 2 * D), D ** -0.5),
                                nrm((N_A, D, DA_HEADS * DA_V_DIM), D ** -0.5 * DEEP_BETA)], axis=-1)
    da_w_o = nrm((N_A, DA_HEADS * DA_V_DIM, D), D ** -0.5 * DEEP_BETA)
    da_lam_q1 = nrm((N_A, DA_HEAD_DIM), 0.1)
    da_lam_k1 = nrm((N_A, DA_HEAD_DIM), 0.1)
    da_lam_q2 = nrm((N_A, DA_HEAD_DIM), 0.1)
    da_lam_k2 = nrm((N_A, DA_HEAD_DIM), 0.1)
    da_subln_g = 1.0 + nrm((N_A, DA_V_DIM), 0.02)
    gla_w_in = jnp.concatenate([nrm((N_B, D, 2 * GLA_HEADS * GLA_DK), D ** -0.5),
                                nrm((N_B, D, GLA_HEADS * GLA_DV), D ** -0.5 * DEEP_BETA),
                                nrm((N_B, D, GLA_HEADS * GLA_DV), D ** -0.5)], axis=-1)
    gla_w_a1 = nrm((N_B, D, GLA_GATE_RANK), D ** -0.5)
    gla_w_a2 = nrm((N_B, GLA_GATE_RANK, GLA_HEADS * GLA_DK), GLA_GATE_RANK ** -0.5)
    gla_b_a = nrm((N_B, GLA_HEADS * GLA_DK), 0.1)
    gla_norm_g = 1.0 + nrm((N_B, GLA_DV), 0.02)
    gla_w_o = nrm((N_B, GLA_HEADS * GLA_DV, D), D ** -0.5 * DEEP_BETA)
    rw_mix = jax.random.uniform(next(ks), (N_C, 6, D), jnp.float32)
    rw_w_rkv = nrm((N_C, 3, D, D), D ** -0.5) * jnp.array([1.0, 1.0, DEEP_BETA], jnp.float32)[None, :, None, None]
    rw_w0 = -1.0 + nrm((N_C, D), 0.5)
    rw_w1 = nrm((N_C, D, RW_DECAY_LORA), D ** -0.5)
    rw_w2 = nrm((N_C, RW_DECAY_LORA, D), 0.1)
    rw_a0 = nrm((N_C, D), 0.1)
    rw_a1 = nrm((N_C, D, RW_A_LORA), D ** -0.5)
    rw_a2 = nrm((N_C, RW_A_LORA, D), 0.1)
    rw_g1 = nrm((N_C, D, RW_GATE_LORA), D ** -0.5)
    rw_g2 = nrm((N_C, RW_GATE_LORA, D), RW_GATE_LORA ** -0.5)
    rw_k_k = 0.85 + nrm((N_C, D), 0.02)
    rw_k_a = 1.0 + nrm((N_C, D), 0.02)
    rw_r_k = nrm((N_C, RW_HEADS, RW_HEAD), 0.1)
    rw_ln_g = 1.0 + nrm((N_C, D), 0.02)
    rw_ln_b = nrm((N_C, D), 0.02)
    rw_w_o = nrm((N_C, D, D), D ** -0.5 * DEEP_BETA)
    return {'x': x, 'p': p, 'rel_bias': rel_bias,
            'ln1_g': ln1_g, 'ln1_b': ln1_b, 'ln2_g': ln2_g, 'ln2_b': ln2_b,
            'mlp_up': mlp_up, 'mlp_down': mlp_down, 'ple_proj': ple_proj, 'ple_gate': ple_gate,
            'da_w_qkv': da_w_qkv, 'da_w_o': da_w_o, 'da_lam_q1': da_lam_q1, 'da_lam_k1': da_lam_k1,
            'da_lam_q2': da_lam_q2, 'da_lam_k2': da_lam_k2, 'da_subln_g': da_subln_g,
            'gla_w_in': gla_w_in, 'gla_w_a1': gla_w_a1, 'gla_w_a2': gla_w_a2, 'gla_b_a': gla_b_a,
            'gla_norm_g': gla_norm_g, 'gla_w_o': gla_w_o,
            'rw_mix': rw_mix, 'rw_w_rkv': rw_w_rkv, 'rw_w0': rw_w0, 'rw_w1': rw_w1, 'rw_w2': rw_w2,
            'rw_a0': rw_a0, 'rw_a1': rw_a1, 'rw_a2': rw_a2, 'rw_g1': rw_g1, 'rw_g2': rw_g2,
            'rw_k_k': rw_k_k, 'rw_k_a': rw_k_a, 'rw_r_k': rw_r_k, 'rw_ln_g': rw_ln_g, 'rw_ln_b': rw_ln_b,
            'rw_w_o': rw_w_o}


def reference(x, p, rel_bias, ln1_g, ln1_b, ln2_g, ln2_b, mlp_up, mlp_down, ple_proj, ple_gate,
              da_w_qkv, da_w_o, da_lam_q1, da_lam_k1, da_lam_q2, da_lam_k2, da_subln_g,
              gla_w_in, gla_w_a1, gla_w_a2, gla_b_a, gla_norm_g, gla_w_o,
              rw_mix, rw_w_rkv, rw_w0, rw_w1, rw_w2, rw_a0, rw_a1, rw_a2, rw_g1, rw_g2,
              rw_k_k, rw_k_a, rw_r_k, rw_ln_g, rw_ln_b, rw_w_o):
    h = x
    for i in range(DEPTH):
        kind, j = i % N_MIXERS, i // N_MIXERS
        if kind == 0:
            lam_init = 0.8 - 0.6 * math.exp(-0.3 * i)
            mixed = diff_attention(h, da_w_qkv[j], da_w_o[j], da_lam_q1[j], da_lam_k1[j],
                                   da_lam_q2[j], da_lam_k2[j], da_subln_g[j], rel_bias, lam_init)
        elif kind == 1:
            mixed = gla(h, gla_w_in[j], gla_w_a1[j], gla_w_a2[j], gla_b_a[j], gla_norm_g[j], gla_w_o[j])
        else:
            mixed = rwkv7_time_mix(h, rw_mix[j], rw_w_rkv[j], rw_w0[j], rw_w1[j], rw_w2[j],
                                   rw_a0[j], rw_a1[j], rw_a2[j], rw_g1[j], rw_g2[j],
                                   rw_k_k[j], rw_k_a[j], rw_r_k[j], rw_ln_g[j], rw_ln_b[j], rw_w_o[j])
        h = layer_norm(DEEP_ALPHA * h + mixed, ln1_g[i], ln1_b[i], LN_EPS)
        h = layer_norm(DEEP_ALPHA * h + sq_relu_mlp(h, mlp_up[i], mlp_down[i]), ln2_g[i], ln2_b[i], LN_EPS)
        h = h + jax.nn.sigmoid(h @ ple_gate[i]) * (p[i] @ ple_proj[i])
    return h
```

```python
import math
import numpy as np
from contextlib import ExitStack
import concourse.bass as bass
import concourse.mybir as mybir
from concourse.bass_utils import run_bass_kernel_spmd

F32 = mybir.dt.float32
BF16 = mybir.dt.bfloat16
AF = mybir.ActivationFunctionType
ALU = mybir.AluOpType
AX = mybir.AxisListType

ENGS = ["pe", "act", "dve", "pool", "sp"]
ARENA_F32 = 53000

D = 1024
DFF = 4096
DEPTH = 4
SEQ = 4096
BATCH = 4
PLE = 256
DEEP_ALPHA = (2.0 * DEPTH) ** 0.25
LN_EPS = 1e-5


class Tile:
    def __init__(self, h, name, space="sb"):
        self.h = h
        self.name = name
        self.space = space
        self.last_w = []
        self.reads = {}
        self.dsem = None
        self.ssem = None

    def __getitem__(self, idx):
        return self.h[idx]

    def bf(self):
        return self.h.bitcast(BF16)

    def __getattr__(self, a):
        return getattr(self.h, a)


class Prog:
    def __init__(self, nc):
        self.nc = nc
        self.es = ExitStack()
        self.ops = {e: [] for e in ENGS}
        self.cnt = {}
        self.seen = {e: {} for e in ENGS}
        self.semh = {}
        self.nsem = 0
        self.ntile = 0
        self.free_dsems = []
        self.scope_dsems = []
        for e in ENGS:
            self._sem(("eng", e))
        self.out_tiles = []
        self.arena = self.es.enter_context(nc.sbuf_tensor("arena", [128, ARENA_F32], F32))
        self.psum = self.es.enter_context(nc.psum_tensor("psum", [128, 4096], F32))
        self.top = 0
        self.hi = ARENA_F32
        self.banks = [Tile(self.psum[:, i * 512:(i + 1) * 512], "bank%d" % i) for i in range(8)]

    def _sem(self, key):
        if key not in self.semh:
            self.nsem += 1
            self.semh[key] = self.es.enter_context(self.nc.semaphore("s%d" % self.nsem))
            self.cnt[key] = 0
        return self.semh[key]

    def sb(self, shape, dtype=F32, name=None, high=False):
        self.ntile += 1
        name = name or ("t%d" % self.ntile)
        p = shape[0]
        n = int(np.prod(shape[1:]))
        words = n if dtype == F32 else (n + 1) // 2
        words = (words + 15) // 16 * 16
        if high:
            self.hi -= words
            off = self.hi
        else:
            off = self.top
            self.top += words
        assert self.top <= self.hi, "SBUF arena overflow: top=%d hi=%d" % (self.top, self.hi)
        ap = self.arena[0:p, off:off + words]
        if dtype != F32:
            ap = ap.bitcast(dtype)
        ap = ap[:, 0:n]
        if len(shape) == 3:
            ap = ap.rearrange("p (a b) -> p a b", a=shape[1])
        elif len(shape) == 4:
            ap = ap.rearrange("p (a b c) -> p a b c", a=shape[1], b=shape[2])
        return Tile(ap, name)

    def dram(self, name, shape, dtype, kind="Internal"):
        t = self.nc.dram_tensor(name, list(shape), dtype, kind=kind)
        tl = Tile(t.ap(), name, space="dram")
        if kind == "ExternalOutput":
            self.out_tiles.append(tl)
        return tl

    def mark(self):
        return self.top

    def free_high(self):
        self.hi = ARENA_F32

    def release(self, mark):
        self.barrier()
        self.top = mark
        self.free_dsems.extend(self.scope_dsems)
        self.scope_dsems = []

    def barrier(self):
        keys = list(self.cnt.keys())
        for e in ENGS:
            waits = []
            for k in keys:
                c = self.cnt[k]
                if c > 0 and self.seen[e].get(k, 0) < c:
                    self.seen[e][k] = c
                    waits.append((self._sem(k), c))

            def emit(eng, waits=waits):
                for s, c in waits:
                    eng.wait_ge(s, c)
            self.ops[e].append(emit)
        for b in self.banks:
            b.last_w = []
            b.reads = {}

    def _need(self, eng, waits, dep):
        if dep is None:
            return
        key, c = dep
        if key == ("eng", "pe") and eng == "pe":
            return
        if self.seen[eng].get(key, 0) >= c:
            return
        waits[key] = max(waits.get(key, 0), c)

    def _deps(self, eng, reads, writes, dma_out=None):
        waits = {}
        for t in reads:
            for dep in t.last_w:
                self._need(eng, waits, dep)
        for t in writes:
            dma_only = all(k[0] == "dma" for k, _ in t.last_w)
            if not (dma_out is not None and t is dma_out and dma_only and not t.reads):
                for dep in t.last_w:
                    self._need(eng, waits, dep)
            for k, c in t.reads.items():
                self._need(eng, waits, (k, c))
        for k, c in waits.items():
            self.seen[eng][k] = c
        return [(self._sem(k), c) for k, c in waits.items()]

    def op(self, eng, fn, reads=(), writes=(), inc=True):
        waits = self._deps(eng, reads, writes)
        key = ("eng", eng)
        if inc:
            self.cnt[key] += 1
            done = self.cnt[key]
        else:
            done = self.cnt[key] + 1
        sem = self._sem(key)

        def emit(e, fn=fn, waits=waits, inc=inc, sem=sem):
            for s, c in waits:
                e.wait_ge(s, c)
            ins = fn(e)
            if inc:
                ins.then_inc(sem, 1)
        self.ops[eng].append(emit)
        for t in reads:
            t.reads[key] = max(t.reads.get(key, 0), done)
        for t in writes:
            t.last_w = [(key, done)]
            t.reads = {}

    def _new_dsem(self, persistent):
        if self.free_dsems and not persistent:
            k = self.free_dsems.pop()
        else:
            k = ("dma", len(self.semh))
            self._sem(k)
        if not persistent:
            self.scope_dsems.append(k)
        return k

    def dma(self, q, out_t, out_ap, in_t, in_ap, persistent=False):
        is_store = in_t is not None and in_t.space == "sb" and out_t.space == "dram"
        if is_store:
            if in_t.ssem is None:
                in_t.ssem = self._new_dsem(False)
            key = in_t.ssem
        else:
            if out_t.dsem is None:
                out_t.dsem = self._new_dsem(persistent)
            key = out_t.dsem
        waits = self._deps(q, [in_t] if in_t is not None else [], [out_t], dma_out=out_t)
        self.cnt[key] += 16
        done = self.cnt[key]
        sem = self._sem(key)

        def emit(e, waits=waits, sem=sem, out_ap=out_ap, in_ap=in_ap):
            for s, c in waits:
                e.wait_ge(s, c)
            e.dma_start(out=out_ap, in_=in_ap).then_inc(sem, 16)
        self.ops[q].append(emit)
        if in_t is not None:
            in_t.reads[key] = max(in_t.reads.get(key, 0), done)
        if out_t.reads or not all(k[0] == "dma" for k, _ in out_t.last_w):
            out_t.last_w = [(key, done)]
        else:
            out_t.last_w = [(k, c) for k, c in out_t.last_w if k != key] + [(key, done)]
        out_t.reads = {}

    def allgather(self, out_t, in_t, groups):
        q = "pool"
        if out_t.dsem is None:
            out_t.dsem = ("dma", len(self.semh))
            self._sem(out_t.dsem)
        waits = self._deps(q, [in_t], [out_t])
        self.cnt[out_t.dsem] += 16
        done = self.cnt[out_t.dsem]
        sem = self._sem(out_t.dsem)

        def emit(e, waits=waits, sem=sem):
            for s_, c in waits:
                e.wait_ge(s_, c)
            e.collective_compute("AllGather", ALU.bypass, replica_groups=groups, ins=[in_t[:]], outs=[out_t[:]]).then_inc(sem, 16)
        self.ops[q].append(emit)
        in_t.reads[out_t.dsem] = max(in_t.reads.get(out_t.dsem, 0), done)
        out_t.last_w = [(out_t.dsem, done)]
        out_t.reads = {}

    def finish(self):
        self.barrier()
        with self.nc.Block() as block:
            for name, deco in (("sp", block.sync), ("pe", block.tensor), ("act", block.scalar),
                               ("dve", block.vector), ("pool", block.gpsimd)):
                lst = self.ops[name]

                def body(e, lst=lst):
                    for f in lst:
                        f(e)
                deco(body)
        self.es.close()


def load_weight_bf16(P, w_t, w_ap, K, N, name=None, high=False, deferred=None):
    kc = K // 128
    t = P.sb([128, kc, N], BF16, name, high=high)
    src = w_ap.rearrange("(c p) n -> p c n", p=128)
    step = max(1, 4096 // N)
    thunks = []
    for c0 in range(0, kc, step):
        c1 = min(kc, c0 + step)
        thunks.append(lambda c0=c0, c1=c1: P.dma("pool", t, t[:, c0:c1, :], w_t, src[:, c0:c1, :], persistent=high))
    if deferred is not None:
        deferred.extend(thunks)
    else:
        for f in thunks:
            f()
    return t


def load_bcast(P, v_t, v_ap, n, name=None):
    t = P.sb([128, n], F32, name)
    P.dma("sp", t, t[:], v_t, v_ap.partition_broadcast(128))
    return t


def layer_norm_tile(P, z, y, g_bc, b_bc, scr, eps=LN_EPS, n=1024, y16=None):
    nk = n // 512
    for k in range(nk):
        P.op("dve", lambda e, k=k: e.bn_stats(out=scr[:, 6 * k:6 * k + 6], in_=z[:, k * 512:(k + 1) * 512]),
             reads=[z], writes=[scr])
    P.op("dve", lambda e: e.bn_aggr(out=scr[:, 12:14], in_=scr[:, 0:6 * nk]), reads=[scr], writes=[scr])
    P.op("act", lambda e: e.activation(out=scr[:, 15:16], in_=scr[:, 13:14], func=AF.Sqrt, bias=eps), reads=[scr], writes=[scr])
    P.op("dve", lambda e: e.reciprocal(out=scr[:, 14:15], in_=scr[:, 15:16]), reads=[scr], writes=[scr])
    P.op("dve", lambda e: e.scalar_tensor_tensor(out=scr[:, 11:12], in0=scr[:, 12:13], scalar=-1.0, in1=scr[:, 14:15], op0=ALU.mult, op1=ALU.mult),
         reads=[scr], writes=[scr])
    P.op("act", lambda e: e.activation(out=y[:], in_=z[:], func=AF.Identity, scale=scr[:, 14:15], bias=scr[:, 11:12]), reads=[z, scr], writes=[y])
    P.op("pool", lambda e: e.tensor_tensor(out=y[:], in0=y[:], in1=g_bc[:], op=ALU.mult), reads=[y, g_bc], writes=[y])
    P.op("dve", lambda e: e.tensor_tensor(out=y[:], in0=y[:], in1=b_bc[:], op=ALU.add), reads=[y, b_bc], writes=[y])
    if y16 is not None:
        P.op("act", lambda e: e.activation(out=y16[:], in_=y[:], func=AF.Copy), reads=[y], writes=[y16])


def ln_part1(P, z, y, g_bc, scr, eps=LN_EPS, n=1024):
    nk = n // 512
    for k in range(nk):
        P.op("dve", lambda e, k=k: e.bn_stats(out=scr[:, 6 * k:6 * k + 6], in_=z[:, k * 512:(k + 1) * 512]), reads=[z], writes=[scr])
    P.op("dve", lambda e: e.bn_aggr(out=scr[:, 12:14], in_=scr[:, 0:6 * nk]), reads=[scr], writes=[scr])
    P.op("act", lambda e: e.activation(out=scr[:, 15:16], in_=scr[:, 13:14], func=AF.Sqrt, bias=eps), reads=[scr], writes=[scr])
    P.op("dve", lambda e: e.reciprocal(out=scr[:, 14:15], in_=scr[:, 15:16]), reads=[scr], writes=[scr])
    P.op("dve", lambda e: e.scalar_tensor_tensor(out=scr[:, 11:12], in0=scr[:, 12:13], scalar=-1.0, in1=scr[:, 14:15], op0=ALU.mult, op1=ALU.mult),
         reads=[scr], writes=[scr])
    P.op("act", lambda e: e.activation(out=y[:], in_=z[:], func=AF.Identity, scale=scr[:, 14:15], bias=scr[:, 11:12]), reads=[z, scr], writes=[y])
    P.op("pool", lambda e: e.tensor_tensor(out=y[:], in0=y[:], in1=g_bc[:], op=ALU.mult), reads=[y, g_bc], writes=[y])


def ln_part2(P, y, b_bc, y16):
    P.op("dve", lambda e: e.tensor_tensor(out=y[:], in0=y[:], in1=b_bc[:], op=ALU.add), reads=[y, b_bc], writes=[y])
    P.op("act", lambda e: e.activation(out=y16[:], in_=y[:], func=AF.Copy), reads=[y], writes=[y16])


def transpose_to(P, src_t, src_bf, nchunk, ident, pbank, dst_t, dst_ap, evac="act"):
    pv = pbank.bf()
    for c in range(nchunk):
        P.op("pe", lambda e, c=c: e.transpose(out=pv[:, c * 128:(c + 1) * 128], in_=src_bf[:, c * 128:(c + 1) * 128],
                                              identity=ident[:]),
             reads=[src_t, ident], writes=[pbank], inc=(c == nchunk - 1))
    src = pv[:, 0:nchunk * 128].rearrange("p (c t) -> p c t", c=nchunk)
    if evac == "act":
        P.op("act", lambda e: e.activation(out=dst_ap, in_=src, func=AF.Copy), reads=[pbank], writes=[dst_t])
    else:
        P.op("dve", lambda e: e.tensor_copy(out=dst_ap, in_=src), reads=[pbank], writes=[dst_t])


def phase_Ra(P, NT, o_d, h_d, wo_d, g_d, b_d, ident_d, out_d, outb_d, prefetch=None):
    m = P.mark()
    nt = NT // 128
    ident = P.sb([128, 128], BF16)
    P.dma("pool", ident, ident[:], ident_d, ident_d[:])
    wo = load_weight_bf16(P, wo_d, wo_d[:], D, D)
    g_bc = load_bcast(P, g_d, g_d[:], D)
    b_bc = load_bcast(P, b_d, b_d[:], D)
    pf = prefetch() if prefetch is not None else []
    NB = 3
    h_f = [P.sb([128, D], F32) for _ in range(NB)]
    o_b = [P.sb([128, D], BF16) for _ in range(NB)]
    oT = [P.sb([128, 8, 128], BF16) for _ in range(NB)]
    z = [P.sb([128, D], F32) for _ in range(NB)]
    y = [P.sb([128, D], F32) for _ in range(NB)]
    y16 = [P.sb([128, D], BF16) for _ in range(NB)]
    scr = [P.sb([128, 16], F32) for _ in range(NB)]
    pT = [P.banks[0], P.banks[3]]
    pm = [[P.banks[1], P.banks[2]], [P.banks[4], P.banks[5]]]

    def load(t):
        i = t % NB
        P.dma("sp", o_b[i], o_b[i][:], o_d, o_d[t * 128:(t + 1) * 128, :])
        P.dma("sp", h_f[i], h_f[i][:], h_d, h_d[t * 128:(t + 1) * 128, :])

    def a0(t):
        i = t % NB
        transpose_to(P, o_b[i], o_b[i], 8, ident, pT[t % 2], oT[i], oT[i][:])

    def a1(t):
        i = t % NB
        for n in range(2):
            pmn = pm[t % 2][n]
            for c in range(8):
                P.op("pe", lambda e, n=n, c=c, i=i, pmn=pmn: e.matmul(pmn[:], lhsT=oT[i][:, c, :], rhs=wo[:, c, n * 512:(n + 1) * 512],
                                                                     start=(c == 0), stop=(c == 7)),
                     reads=[oT[i], wo], writes=[pmn], inc=(c == 7))
            P.op("dve", lambda e, n=n, i=i, pmn=pmn: e.scalar_tensor_tensor(out=z[i][:, n * 512:(n + 1) * 512], in0=h_f[i][:, n * 512:(n + 1) * 512],
                                                                           scalar=DEEP_ALPHA, in1=pmn[:], op0=ALU.mult, op1=ALU.add),
                 reads=[h_f[i], pmn], writes=[z[i]])

    def b1(t):
        i = t % NB
        ln_part1(P, z[i], y[i], g_bc, scr[i])

    def b2(t):
        i = t % NB
        ln_part2(P, y[i], b_bc, y16[i])
        P.dma("pool", out_d, out_d[t * 128:(t + 1) * 128, :], y[i], y[i][:], persistent=True)
        P.dma("pool", outb_d, outb_d[t * 128:(t + 1) * 128, :], y16[i], y16[i][:], persistent=True)
        npf = (len(pf) + max(1, nt - t) - 1) // max(1, nt - t) if t % 2 == 1 or nt - t <= len(pf) else 0
        for _ in range(min(npf, len(pf))):
            pf.pop(0)()

    load(0)
    if nt > 1:
        load(1)
    for k in range(-1, nt + 2):
        if 2 <= k + 2 < nt:
            load(k + 2)
        if 0 <= k + 1 < nt:
            a0(k + 1)
        if 0 <= k < nt:
            a1(k)
        if 0 <= k - 1 < nt:
            b1(k - 1)
        if 0 <= k - 2 < nt:
            b2(k - 2)
    while pf:
        pf.pop(0)()
    P.release(m)


def phase_Rb(P, NT, h1_d, h1b_d, wup_d, wdn_d, g_d, b_d, ident_d, out_d, outb_d, pre=None):
    m = P.mark()
    G = min(512, NT)
    ng = NT // G
    nj = G // 128
    ident = P.sb([128, 128], BF16)
    P.dma("pool", ident, ident[:], ident_d, ident_d[:])
    if pre is not None:
        wup, wdn = pre
    else:
        wup = load_weight_bf16(P, wup_d, wup_d[:], D, DFF)
        wdn = load_weight_bf16(P, wdn_d, wdn_d[:], DFF, D)
    g_bc = load_bcast(P, g_d, g_d[:], D)
    b_bc = load_bcast(P, b_d, b_d[:], D)
    ht = [P.sb([128, D], BF16) for _ in range(2)]
    hr = [P.sb([128, D], F32) for _ in range(2)]
    y16 = P.sb([128, D], BF16)
    h1T = P.sb([128, 8, G], BF16)
    aT = P.sb([128, 32, G], BF16)
    z = P.sb([128, D], F32)
    yy = P.sb([128, D], F32)
    scr = P.sb([128, 16], F32)
    relu_t = [P.sb([128, 512], F32) for _ in range(2)]
    pT = P.banks[0]
    pu = [P.banks[1], P.banks[2], P.banks[3]]
    pd = [P.banks[4], P.banks[5]]
    k = 0
    for g in range(ng):
        for j in range(nj):
            r0 = g * G + j * 128
            t = ht[k % 2]
            k += 1
            P.dma("sp", t, t[:], h1b_d, h1b_d[r0:r0 + 128, :])
            transpose_to(P, t, t, 8, ident, pT, h1T, h1T[:, :, j * 128:(j + 1) * 128])
        for f in range(32):
            pb = pu[f % 3]
            for c in range(8):
                P.op("pe", lambda e, f=f, c=c, pb=pb: e.matmul(pb[:, 0:G], lhsT=wup[:, c, f * 128:(f + 1) * 128], rhs=h1T[:, c, :],
                                                              start=(c == 0), stop=(c == 7)),
                     reads=[wup, h1T], writes=[pb], inc=(c == 7))
            rl = relu_t[f % 2]
            P.op("act", lambda e, pb=pb, rl=rl: e.activation(out=rl[:, 0:G], in_=pb[:, 0:G], func=AF.Relu), reads=[pb], writes=[rl])
            sq_eng = "pool" if f % 3 == 2 else "dve"
            P.op(sq_eng, lambda e, f=f, rl=rl: e.tensor_tensor(out=aT[:, f, :], in0=rl[:, 0:G], in1=rl[:, 0:G], op=ALU.mult),
                 reads=[rl], writes=[aT])
        for j in range(nj):
            r0 = g * G + j * 128
            hres = hr[j % 2]
            P.dma("sp", hres, hres[:], h1_d, h1_d[r0:r0 + 128, :])
            for n in range(2):
                for f in range(32):
                    P.op("pe", lambda e, n=n, f=f, j=j: e.matmul(pd[n][:], lhsT=aT[:, f, j * 128:(j + 1) * 128], rhs=wdn[:, f, n * 512:(n + 1) * 512],
                                                                start=(f == 0), stop=(f == 31)),
                         reads=[aT, wdn], writes=[pd[n]], inc=(f == 31))
                P.op("dve", lambda e, n=n, hres=hres: e.scalar_tensor_tensor(out=z[:, n * 512:(n + 1) * 512], in0=hres[:, n * 512:(n + 1) * 512],
                                                                           scalar=DEEP_ALPHA, in1=pd[n][:], op0=ALU.mult, op1=ALU.add),
                     reads=[hres, pd[n]], writes=[z])
            layer_norm_tile(P, z, yy, g_bc, b_bc, scr, y16=y16)
            P.dma("pool", out_d, out_d[r0:r0 + 128, :], yy, yy[:], persistent=True)
            P.dma("pool", outb_d, outb_d[r0:r0 + 128, :], y16, y16[:], persistent=True)
    P.release(m)


def phase_Rc(P, NT, h2_d, h2b_d, p_d, wg_d, wp_d, ident_d, out_d, outb_d):
    m = P.mark()
    nt = NT // 128
    ident = P.sb([128, 128], BF16)
    P.dma("pool", ident, ident[:], ident_d, ident_d[:])
    wg = load_weight_bf16(P, wg_d, wg_d[:], D, D)
    wp = load_weight_bf16(P, wp_d, wp_d[:], PLE, D)
    NB = 3
    NH_ = 5
    hf = [P.sb([128, D], F32) for _ in range(NH_)]
    pf = [P.sb([128, PLE], F32) for _ in range(NB)]
    hb = [P.sb([128, D], BF16) for _ in range(NB)]
    pb16 = [P.sb([128, PLE], BF16) for _ in range(NB)]
    hT = [P.sb([128, 10, 128], BF16) for _ in range(NB)]
    sg = [P.sb([128, D], F32) for _ in range(NB)]
    y = [P.sb([128, D], F32) for _ in range(NB)]
    y16 = [P.sb([128, D], BF16) for _ in range(NB)]
    ppv = [P.sb([128, D], F32) for _ in range(NB)]
    pT = [P.banks[0], P.banks[1]]
    pg = [[P.banks[2], P.banks[3]], [P.banks[4], P.banks[5]]]
    pp = [P.banks[6], P.banks[7]]

    def load(t):
        i = t % NB
        P.dma("sp", hf[t % NH_], hf[t % NH_][:], h2_d, h2_d[t * 128:(t + 1) * 128, :])
        P.dma("sp", hb[i], hb[i][:], h2b_d, h2b_d[t * 128:(t + 1) * 128, :])
        P.dma("sp", pf[i], pf[i][:], p_d, p_d[t * 128:(t + 1) * 128, :])

    def a0(t):
        i = t % NB
        P.op("act", lambda e, i=i: e.activation(out=pb16[i][:], in_=pf[i][:], func=AF.Copy), reads=[pf[i]], writes=[pb16[i]])
        transpose_to(P, hb[i], hb[i], 8, ident, pT[0], hT[i], hT[i][:, 0:8, :])
        transpose_to(P, pb16[i], pb16[i], 2, ident, pT[1], hT[i], hT[i][:, 8:10, :], evac="dve")

    def a1(t):
        i = t % NB
        for n in range(2):
            pgn, ppn = pg[t % 2][n], pp[n]
            for c in range(8):
                P.op("pe", lambda e, n=n, c=c, i=i, pgn=pgn: e.matmul(pgn[:], lhsT=hT[i][:, c, :], rhs=wg[:, c, n * 512:(n + 1) * 512],
                                                                     start=(c == 0), stop=(c == 7)),
                     reads=[hT[i], wg], writes=[pgn], inc=(c == 7))
            for c in range(2):
                P.op("pe", lambda e, n=n, c=c, i=i, ppn=ppn: e.matmul(ppn[:, 0:512], lhsT=hT[i][:, 8 + c, :], rhs=wp[:, c, n * 512:(n + 1) * 512],
                                                                     start=(c == 0), stop=(c == 1)),
                     reads=[hT[i], wp], writes=[ppn], inc=(c == 1))
            P.op("dve", lambda e, n=n, i=i, ppn=ppn: e.tensor_copy(out=ppv[i][:, n * 512:(n + 1) * 512], in_=ppn[:, 0:512]), reads=[ppn], writes=[ppv[i]])

    def b1(t):
        i = t % NB
        for n in range(2):
            pgn = pg[t % 2][n]
            P.op("act", lambda e, n=n, i=i, pgn=pgn: e.activation(out=sg[i][:, n * 512:(n + 1) * 512], in_=pgn[:], func=AF.Sigmoid),
                 reads=[pgn], writes=[sg[i]])
        P.op("pool", lambda e, i=i: e.tensor_tensor(out=sg[i][:], in0=sg[i][:], in1=ppv[i][:], op=ALU.mult), reads=[sg[i], ppv[i]], writes=[sg[i]])

    def b2(t):
        i = t % NB
        hft = hf[t % NH_]
        P.op("dve", lambda e, i=i, hft=hft: e.tensor_tensor(out=y[i][:], in0=sg[i][:], in1=hft[:], op=ALU.add),
             reads=[sg[i], hft], writes=[y[i]])
        P.op("act", lambda e, i=i: e.activation(out=y16[i][:], in_=y[i][:], func=AF.Copy), reads=[y[i]], writes=[y16[i]])
        P.dma("pool", out_d, out_d[t * 128:(t + 1) * 128, :], y[i], y[i][:], persistent=True)
        P.dma("pool", outb_d, outb_d[t * 128:(t + 1) * 128, :], y16[i], y16[i][:], persistent=True)

    load(0)
    if nt > 1:
        load(1)
    for k in range(-1, nt + 2):
        if 2 <= k + 2 < nt:
            load(k + 2)
        if 0 <= k + 1 < nt:
            a0(k + 1)
        if 0 <= k < nt:
            a1(k)
        if 0 <= k - 1 < nt:
            b1(k - 1)
        if 0 <= k - 2 < nt:
            b2(k - 2)
    P.release(m)


def phase_X(P, S, x_d, xb_d):
    m = P.mark()
    xf = [P.sb([128, 4, D], F32) for _ in range(2)]
    xb = [P.sb([128, 4, D], BF16) for _ in range(2)]
    for g in range(S // 512):
        i = g % 2
        P.dma("sp", xf[i], xf[i][:], x_d, x_d[g * 512:(g + 1) * 512, :].rearrange("(j p) n -> p j n", p=128))
        if g % 2 == 0:
            P.op("act", lambda e, i=i: e.activation(out=xb[i][:], in_=xf[i][:], func=AF.Copy), reads=[xf[i]], writes=[xb[i]])
        else:
            P.op("dve", lambda e, i=i: e.tensor_copy(out=xb[i][:], in_=xf[i][:]), reads=[xf[i]], writes=[xb[i]])
        P.dma("pool", xb_d, xb_d[g * 512:(g + 1) * 512, :].rearrange("(j p) n -> p j n", p=128), xb[i], xb[i][:], persistent=True)
    P.release(m)


def phase_A(P, S, NH, lam_init, h_d, w3_d, btab_d, mask_d, b31_d, lam_d, subg_d, ident_d, o_d, ocol=0):
    m0 = P.mark()
    HW = NH * 128
    nqt = S // 512
    nblk = S // 128
    ident = P.sb([128, 128], BF16)
    P.dma("pool", ident, ident[:], ident_d, ident_d[:])
    w3 = load_weight_bf16(P, w3_d, w3_d[:], D, 3 * HW)
    qT = P.sb([128, NH, S], BF16)
    kTz = [P.sb([128, NH, S], BF16) for _ in range(2)]
    P.op("pool", lambda e: e.memset(kTz[0][64:128, :, :], 0.0), writes=[kTz[0]])
    P.op("pool", lambda e: e.memset(kTz[1][0:64, :, :], 0.0), writes=[kTz[1]])
    V = P.sb([128, nblk, NH, 130], BF16)
    P.op("pool", lambda e: e.memset(V[:, :, :, 128:130], 1.0), writes=[V])

    bt = P.sb([128, NH, 2, 128], F32)
    P.dma("sp", bt, bt[:], btab_d, btab_d.rearrange("h t k q -> k h t q"))
    mask = P.sb([128, 128], F32)
    P.dma("sp", mask, mask[:], mask_d, mask_d[:])
    mneg = P.sb([128, 128], F32)
    P.op("dve", lambda e: e.tensor_scalar(out=mneg[:], in0=mask[:], scalar1=30000.0, scalar2=-30000.0, op0=ALU.mult, op1=ALU.add),
         reads=[mask], writes=[mneg])
    for hd in range(NH):
        P.op("dve", lambda e, hd=hd: e.tensor_tensor(out=bt[:, hd, 0, :], in0=bt[:, hd, 0, :], in1=mask[:], op=ALU.mult),
             reads=[bt, mask], writes=[bt])
        P.op("dve", lambda e, hd=hd: e.tensor_tensor(out=bt[:, hd, 0, :], in0=bt[:, hd, 0, :], in1=mneg[:], op=ALU.add),
             reads=[bt, mneg], writes=[bt])
    b31 = load_bcast(P, b31_d, b31_d[:], NH)
    lamv = P.sb([128, 4, 64], F32)
    P.dma("sp", lamv, lamv[:], lam_d, lam_d.rearrange("a d -> (a d)").partition_broadcast(128).rearrange("p (a d) -> p a d", a=4))
    lsc = P.sb([128, 8], F32)
    ltmp = P.sb([128, 2, 64], F32)
    P.op("dve", lambda e: e.tensor_tensor(out=ltmp[:, 0, :], in0=lamv[:, 0, :], in1=lamv[:, 1, :], op=ALU.mult), reads=[lamv], writes=[ltmp])
    P.op("dve", lambda e: e.tensor_tensor(out=ltmp[:, 1, :], in0=lamv[:, 2, :], in1=lamv[:, 3, :], op=ALU.mult), reads=[lamv], writes=[ltmp])
    P.op("dve", lambda e: e.tensor_reduce(out=lsc[:, 0:2], in_=ltmp[:], axis=AX.X, op=ALU.add), reads=[ltmp], writes=[lsc])
    P.op("act", lambda e: e.activation(out=lsc[:, 2:4], in_=lsc[:, 0:2], func=AF.Exp), reads=[lsc], writes=[lsc])
    P.op("dve", lambda e: e.tensor_tensor(out=lsc[:, 4:5], in0=lsc[:, 3:4], in1=lsc[:, 2:3], op=ALU.subtract), reads=[lsc], writes=[lsc])
    P.op("dve", lambda e: e.tensor_scalar(out=lsc[:, 5:6], in0=lsc[:, 4:5], scalar1=-lam_init, scalar2=None, op0=ALU.add), reads=[lsc], writes=[lsc])
    subg = load_bcast(P, subg_d, subg_d[:], 128)
    P.op("dve", lambda e: e.tensor_scalar(out=subg[:], in0=subg[:], scalar1=(1.0 - lam_init), scalar2=None, op0=ALU.mult), reads=[subg], writes=[subg])

    m1 = P.mark()
    hf = [P.sb([128, 4, D], BF16) for _ in range(2)]
    hT = P.sb([128, 8, 512], BF16)
    pT = P.banks[0]
    pq = [P.banks[1], P.banks[2], P.banks[3]]

    def load(g):
        P.dma("sp", hf[g % 2], hf[g % 2][:], h_d, h_d[g * 512:(g + 1) * 512, :].rearrange("(j p) n -> p j n", p=128))

    load(0)
    it = 0
    for g in range(nqt):
        if g + 1 < nqt:
            load(g + 1)
        for j in range(4):
            transpose_to(P, hf[g % 2], hf[g % 2][:, j, :], 8, ident, pT, hT, hT[:, :, j * 128:(j + 1) * 128])
        for hd in range(NH):
            for which in range(2):
                pb = pq[it % 3]
                it += 1
                col = which * HW + hd * 128
                for c in range(8):
                    P.op("pe", lambda e, c=c, col=col, pb=pb: e.matmul(pb[:], lhsT=w3[:, c, col:col + 128], rhs=hT[:, c, :],
                                                                      start=(c == 0), stop=(c == 7)),
                         reads=[w3, hT], writes=[pb], inc=(c == 7))
                if which == 0:
                    P.op("act", lambda e, hd=hd, g=g, pb=pb: e.activation(out=qT[:, hd, g * 512:(g + 1) * 512], in_=pb[:], func=AF.Copy, scale=0.125),
                         reads=[pb], writes=[qT])
                else:
                    P.op("dve", lambda e, hd=hd, g=g, pb=pb: e.tensor_copy(out=kTz[0][0:64, hd, g * 512:(g + 1) * 512], in_=pb[0:64, :]),
                         reads=[pb], writes=[kTz[0]])
                    P.op("dve", lambda e, hd=hd, g=g, pb=pb: e.tensor_copy(out=kTz[1][64:128, hd, g * 512:(g + 1) * 512], in_=pb[64:128, :]),
                         reads=[pb], writes=[kTz[1]])
        for j in range(4):
            pb = pq[it % 3]
            it += 1
            for c in range(8):
                P.op("pe", lambda e, c=c, j=j, pb=pb: e.matmul(pb[:, 0:HW], lhsT=hT[:, c, j * 128:(j + 1) * 128], rhs=w3[:, c, 2 * HW:3 * HW],
                                                              start=(c == 0), stop=(c == 7)),
                     reads=[w3, hT], writes=[pb], inc=(c == 7))
            eng = "dve" if j % 2 == 0 else "act"
            src = pb[:, 0:HW].rearrange("p (h d) -> p h d", h=NH)
            if eng == "dve":
                P.op("dve", lambda e, g=g, j=j, src=src: e.tensor_copy(out=V[:, g * 4 + j, :, 0:128], in_=src), reads=[pb], writes=[V])
            else:
                P.op("act", lambda e, g=g, j=j, src=src: e.activation(out=V[:, g * 4 + j, :, 0:128], in_=src, func=AF.Copy), reads=[pb], writes=[V])
    P.release(m1)

    pss = [P.banks[0], P.banks[1], P.banks[6]]
    po = [P.banks[2], P.banks[3], P.banks[4], P.banks[5]]
    pts = [P.sb([128, 512], BF16) for _ in range(4)]
    tmpf = [P.sb([128, 128], F32) for _ in range(3)]
    oacc = [P.sb([128, 4, 132], F32) for _ in range(2)]
    omu = [[P.sb([128, 4, 128], F32) for _ in range(2)] for _ in range(2)]
    rec = P.sb([128, 8], F32)
    pending = []
    ob = [P.sb([128, 4, 128], F32) for _ in range(2)]
    ob16 = [P.sb([128, 4, 128], BF16) for _ in range(2)]
    sq = P.sb([128, 4, 128], F32)
    rs = P.sb([128, 8], F32)
    eps5 = P.sb([128, 1], F32)
    P.op("dve", lambda e: e.memset(eps5[:], 1e-5), writes=[eps5])
    items = [(hd, qt, m, kb) for hd in range(NH) for qt in range(nqt) for m in range(2) for kb in range(4 * qt + 4)]
    AHEAD = 2
    cnt = {"nt": 0, "ne": 0}

    def qk(i):
        hd, qt, m, kb = items[i]
        c0 = max(0, kb - 4 * qt)
        ps = pss[i % 3]
        P.op("pe", lambda e: e.matmul(ps[:, c0 * 128:512], lhsT=kTz[m][:, hd, kb * 128:(kb + 1) * 128],
                                      rhs=qT[:, hd, qt * 512 + c0 * 128:(qt + 1) * 512], start=True, stop=True),
             reads=[kTz[m], qT], writes=[ps])

    def expv(i):
        hd, qt, m, kb = items[i]
        c0 = max(0, kb - 4 * qt)
        ps = pss[i % 3]
        pt = pts[i % 4]
        cfar = max(c0, kb + 2 - 4 * qt)
        for c in range(c0, min(cfar, 4)):
            dlt = 4 * qt + c - kb
            tf = tmpf[cnt["nt"] % 3]
            cnt["nt"] += 1
            P.op("dve", lambda e, c=c, tf=tf, dlt=dlt: e.tensor_tensor(out=tf[:], in0=ps[:, c * 128:(c + 1) * 128], in1=bt[:, hd, dlt, :], op=ALU.add),
                 reads=[ps, bt], writes=[tf])
            P.op("act", lambda e, c=c, tf=tf: e.activation(out=pt[:, c * 128:(c + 1) * 128], in_=tf[:], func=AF.Exp), reads=[tf], writes=[pt])
        if cfar < 4:
            P.op("act", lambda e: e.activation(out=pt[:, cfar * 128:512], in_=ps[:, cfar * 128:512], func=AF.Exp, bias=b31[:, hd:hd + 1]),
                 reads=[ps, b31], writes=[pt])
        for c in range(c0, 4):
            P.op("pe", lambda e, c=c: e.matmul(po[c][:, 0:129], lhsT=pt[:, c * 128:(c + 1) * 128], rhs=V[:, kb, hd, 0:129],
                                               start=(kb == 0), stop=(kb == 4 * qt + c)),
                 reads=[pt, V], writes=[po[c]], inc=(c == 3))
        if kb != 4 * qt + 3:
            return
        oa = oacc[cnt["ne"] % 2]
        cnt["ne"] += 1
        for c in range(4):
            P.op("dve", lambda e, c=c: e.tensor_copy(out=oa[:, c, 0:129], in_=po[c][:, 0:129]), reads=[po[c]], writes=[oa])
        P.op("dve", lambda e: e.reciprocal(out=rec[:, 4 * m:4 * m + 4], in_=oa[:, :, 128]), reads=[oa], writes=[rec])
        u = hd * nqt + qt
        om = omu[u % 2]
        P.op("pool", lambda e: e.tensor_tensor(out=om[m][:], in0=oa[:, :, 0:128], in1=rec[:, 4 * m:4 * m + 4].unsqueeze(2).to_broadcast([128, 4, 128]), op=ALU.mult),
             reads=[oa, rec], writes=[om[m]])
        if m == 0:
            return

        def tail(hd=hd, qt=qt, om=om, u=u):
            o = ob[u % 2]
            o16 = ob16[u % 2]
            P.op("dve", lambda e: e.scalar_tensor_tensor(out=o[:], in0=om[1][:], scalar=lsc[:, 5:6], in1=om[0][:], op0=ALU.mult, op1=ALU.add),
                 reads=[om[0], om[1], lsc], writes=[o])
            P.op("pool", lambda e: e.tensor_tensor(out=sq[:], in0=o[:], in1=o[:], op=ALU.mult), reads=[o], writes=[sq])
            P.op("dve", lambda e: e.tensor_reduce(out=rs[:, 0:4], in_=sq[:], axis=AX.X, op=ALU.add), reads=[sq], writes=[rs])
            P.op("act", lambda e: e.activation(out=rs[:, 0:4], in_=rs[:, 0:4], func=AF.Ln, bias=eps5[:, 0:1], scale=1.0 / 128.0), reads=[rs, eps5], writes=[rs])
            P.op("act", lambda e: e.activation(out=rs[:, 4:8], in_=rs[:, 0:4], func=AF.Exp, scale=-0.5), reads=[rs], writes=[rs])
            P.op("pool", lambda e: e.tensor_tensor(out=o[:], in0=o[:], in1=rs[:, 4:8].unsqueeze(2).to_broadcast([128, 4, 128]), op=ALU.mult),
                 reads=[o, rs], writes=[o])
            P.op("pool", lambda e: e.tensor_tensor(out=o16[:], in0=o[:], in1=subg[:].unsqueeze(1).to_broadcast([128, 4, 128]), op=ALU.mult),
                 reads=[o, subg], writes=[o16])
            P.dma("pool", o_d, o_d[qt * 512:(qt + 1) * 512, ocol + hd * 128:ocol + (hd + 1) * 128].rearrange("(c p) d -> p c d", p=128), o16, o16[:], persistent=True)
        pending.append([3, tail])

    for i in range(len(items) + AHEAD):
        if i < len(items):
            qk(i)
        if i >= AHEAD:
            if pending:
                pending[0][0] -= 1
                if pending[0][0] <= 0:
                    pending.pop(0)[1]()
            expv(i - AHEAD)
    while pending:
        pending.pop(0)[1]()
    P.release(m0)


def t5_bucket_np(rel):
    n = np.maximum(rel, 0)
    nf = np.maximum(n, 1).astype(np.float32)
    large = 16 + (np.log(nf / np.float32(16)) / np.float32(math.log(128 / 16)) * np.float32(16)).astype(np.int32)
    large = np.minimum(large, 31)
    return np.where(n < 16, n, large)


def attn_bias_tables(rel_bias, heads):
    k = np.arange(128)[:, None]
    q = np.arange(128)[None, :]
    idx = np.stack([t5_bucket_np(q - k), t5_bucket_np(128 + q - k)], 0)
    tab = rel_bias[idx]
    return np.ascontiguousarray(np.transpose(tab[..., heads], (3, 0, 1, 2))).astype(np.float32)


def phase_G(P, S, NH, h_d, w4_d, wa1_d, wa2_d, ba_d, ng_d, ident_d, mask_d, tris_d, o_d, ocol=0):
    m0 = P.mark()
    QW = NH * 128
    VW = NH * 256
    nch = S // 128
    ident = P.sb([128, 128], BF16)
    P.dma("pool", ident, ident[:], ident_d, ident_d[:])
    mask = P.sb([128, 128], F32)
    P.dma("sp", mask, mask[:], mask_d, mask_d[:])
    trii = P.sb([128, 128], F32)
    tris = P.sb([128, 128], F32)
    P.dma("sp", tris, tris[:], tris_d, tris_d[:])
    P.op("dve", lambda e: e.tensor_scalar(out=trii[:], in0=mask[:], scalar1=-1.0 / 16.0, scalar2=None, op0=ALU.mult), reads=[mask], writes=[trii])
    P.op("dve", lambda e: e.tensor_scalar(out=tris[:], in0=tris[:], scalar1=-1.0 / 16.0, scalar2=None, op0=ALU.mult), reads=[tris], writes=[tris])
    w4 = load_weight_bf16(P, w4_d, w4_d[:], D, 2 * QW + 2 * VW)
    wa1 = load_weight_bf16(P, wa1_d, wa1_d[:], D, 16)
    wa2 = P.sb([16, QW], BF16)
    P.dma("pool", wa2, wa2[:], wa2_d, wa2_d[:])
    ba = P.sb([1, QW], BF16)
    P.dma("pool", ba, ba[:], ba_d, ba_d.rearrange("(o n) -> o n", o=1))
    ones = P.sb([1, 128], BF16)
    P.op("dve", lambda e: e.memset(ones[:], 1.0), writes=[ones])
    ng = load_bcast(P, ng_d, ng_d[:], 256)
    one1 = P.sb([128, 1], F32)
    P.op("dve", lambda e: e.memset(one1[:], 1.0), writes=[one1])
    eps5 = P.sb([128, 1], F32)
    P.op("dve", lambda e: e.memset(eps5[:], 1e-5), writes=[eps5])
    St = [P.sb([128, 256], F32) for _ in range(NH)]
    Sb = [P.sb([128, 256], BF16) for _ in range(NH)]
    for h in range(NH):
        P.op("dve", lambda e, h=h: e.memset(St[h][:], 0.0), writes=[St[h]])
        P.op("pool", lambda e, h=h: e.memset(Sb[h][:], 0.0), writes=[Sb[h]])
    hf = [P.sb([128, D], BF16) for _ in range(2)]
    hT = P.sb([128, 8, 128], BF16)
    a1T = P.sb([16, 128], BF16)
    la = P.sb([128, QW], F32)
    Eq = P.sb([128, NH, 128], F32)
    Ek = P.sb([128, NH, 128], F32)
    Er = P.sb([128, NH, 128], F32)
    qd = P.sb([128, NH, 128], BF16)
    kd = P.sb([128, NH, 128], BF16)
    kr = P.sb([128, NH, 128], BF16)
    vb = P.sb([128, VW], BF16)
    sr = P.sb([128, VW], F32)
    att = P.sb([128, NH, 128], BF16)
    osb = P.sb([128, NH, 256], F32)
    sq = P.sb([128, NH, 256], F32)
    rs = P.sb([128, 8], F32)
    yo = [P.sb([128, NH, 256], F32) for _ in range(2)]
    yo16 = [P.sb([128, NH, 256], BF16) for _ in range(2)]
    B = P.banks
    pT, pqk, pkt, pz, pv, pr, pcum, po = B[0], B[1], B[2], B[3], B[4], B[5], B[6], B[7]
    patt, pkv = B[0], B[4]

    def load(ci):
        P.dma("sp", hf[ci % 2], hf[ci % 2][:], h_d, h_d[ci * 128:(ci + 1) * 128, :])

    def mm8(pb_t, out_ap, lhs_fn, rhs_fn, reads):
        for c in range(8):
            P.op("pe", lambda e, c=c: e.matmul(out_ap, lhsT=lhs_fn(c), rhs=rhs_fn(c), start=(c == 0), stop=(c == 7)),
                 reads=reads, writes=[pb_t], inc=(c == 7))

    load(0)
    for ci in range(nch):
        if ci + 1 < nch:
            load(ci + 1)
        hfi = hf[ci % 2]
        transpose_to(P, hfi, hfi, 8, ident, pT, hT, hT[:])
        for which in range(2):
            for h in range(NH):
                col = which * QW + h * 128
                slot = (which * NH + h) * 128
                mm8(pqk, pqk[:, slot:slot + 128], lambda c, col=col: w4[:, c, col:col + 128], lambda c: hT[:, c, :], [w4, hT])
        mm8(pkt, pkt[:, 0:QW], lambda c: hT[:, c, :], lambda c: w4[:, c, QW:2 * QW], [w4, hT])
        mm8(pv, pv[:, 0:VW], lambda c: hT[:, c, :], lambda c: w4[:, c, 2 * QW:2 * QW + VW], [w4, hT])
        mm8(pr, pr[:, 0:VW], lambda c: hT[:, c, :], lambda c: w4[:, c, 2 * QW + VW:2 * QW + 2 * VW], [w4, hT])
        mm8(pz, pz[0:16, 384:512], lambda c: wa1[:, c, :], lambda c: hT[:, c, :], [wa1, hT])
        P.op("act", lambda e: e.activation(out=a1T[:], in_=pz[0:16, 384:512], func=AF.Copy), reads=[pz], writes=[a1T])
        P.op("pe", lambda e: e.matmul(pz[:, 0:QW], lhsT=a1T[:], rhs=wa2[:], start=True, stop=False), reads=[a1T, wa2], writes=[pz], inc=False)
        P.op("pe", lambda e: e.matmul(pz[:, 0:QW], lhsT=ones[:], rhs=ba[:], start=False, stop=True), reads=[ones, ba], writes=[pz])
        P.op("act", lambda e: e.activation(out=sr[:], in_=pr[:, 0:VW], func=AF.Sigmoid), reads=[pr], writes=[sr])
        P.op("dve", lambda e: e.tensor_tensor(out=sr[:], in0=sr[:], in1=pr[:, 0:VW], op=ALU.mult), reads=[sr, pr], writes=[sr])
        P.op("act", lambda e: e.activation(out=la[:], in_=pz[:, 0:QW], func=AF.Exp, scale=-1.0), reads=[pz], writes=[la])
        P.op("act", lambda e: e.activation(out=la[:], in_=la[:], func=AF.Ln, bias=one1[:, 0:1]), reads=[la, one1], writes=[la])
        P.op("act", lambda e: e.activation(out=vb[:], in_=pv[:, 0:VW], func=AF.Copy), reads=[pv], writes=[vb])
        for h in range(NH):
            P.op("pe", lambda e, h=h: e.matmul(pcum[:, h * 128:(h + 1) * 128], lhsT=la[:, h * 128:(h + 1) * 128], rhs=trii[:], start=True, stop=True),
                 reads=[la, trii], writes=[pcum], inc=False)
            P.op("pe", lambda e, h=h: e.matmul(pcum[:, (NH + h) * 128:(NH + h + 1) * 128], lhsT=tris[:], rhs=la[:, h * 128:(h + 1) * 128], start=True, stop=True),
                 reads=[la, tris], writes=[pcum], inc=(h == NH - 1))
        cq = pcum[:, 0:NH * 128].rearrange("p (h t) -> p h t", h=NH)
        cr = pcum[:, NH * 128:2 * NH * 128].rearrange("p (h t) -> p h t", h=NH)
        P.op("act", lambda e: e.activation(out=Eq[:], in_=cq, func=AF.Exp), reads=[pcum], writes=[Eq])
        P.op("act", lambda e: e.activation(out=Ek[:], in_=cq, func=AF.Exp, scale=-1.0), reads=[pcum], writes=[Ek])
        P.op("act", lambda e: e.activation(out=Er[:], in_=cr, func=AF.Exp), reads=[pcum], writes=[Er])
        qv = pqk[:, 0:NH * 128].rearrange("p (h t) -> p h t", h=NH)
        kv_ = pqk[:, NH * 128:2 * NH * 128].rearrange("p (h t) -> p h t", h=NH)
        P.op("dve", lambda e: e.scalar_tensor_tensor(out=qd[:], in0=qv, scalar=128.0 ** -0.5, in1=Eq[:], op0=ALU.mult, op1=ALU.mult),
             reads=[pqk, Eq], writes=[qd])
        P.op("dve", lambda e: e.tensor_tensor(out=kd[:], in0=kv_, in1=Ek[:], op=ALU.mult), reads=[pqk, Ek], writes=[kd])
        P.op("dve", lambda e: e.tensor_tensor(out=kr[:], in0=pkt[:, 0:QW].rearrange("p (h t) -> p h t", h=NH), in1=Er[:], op=ALU.mult),
             reads=[pkt, Er], writes=[kr])
        for h in range(NH):
            P.op("pe", lambda e, h=h: e.matmul(patt[:, h * 128:(h + 1) * 128], lhsT=kd[:, h, :], rhs=qd[:, h, :], start=True, stop=True),
                 reads=[kd, qd], writes=[patt], inc=(h == NH - 1))
        P.op("dve", lambda e: e.tensor_tensor(out=att[:], in0=patt[:, 0:NH * 128].rearrange("p (h t) -> p h t", h=NH),
                                              in1=mask[:].unsqueeze(1).to_broadcast([128, NH, 128]), op=ALU.mult),
             reads=[patt, mask], writes=[att])
        for h in range(NH):
            P.op("pe", lambda e, h=h: e.matmul(po[:, h * 256:(h + 1) * 256], lhsT=att[:, h, :], rhs=vb[:, h * 256:(h + 1) * 256], start=True, stop=False),
                 reads=[att, vb], writes=[po], inc=False)
            P.op("pe", lambda e, h=h: e.matmul(po[:, h * 256:(h + 1) * 256], lhsT=qd[:, h, :], rhs=Sb[h][:], start=False, stop=True),
                 reads=[qd, Sb[h]], writes=[po], inc=(h == NH - 1))
        for h in range(NH):
            P.op("pe", lambda e, h=h: e.matmul(pkv[:, h * 256:(h + 1) * 256], lhsT=kr[:, h, :], rhs=vb[:, h * 256:(h + 1) * 256], start=True, stop=True),
                 reads=[kr, vb], writes=[pkv], inc=(h == NH - 1))
        for h in range(NH):
            P.op("dve", lambda e, h=h: e.scalar_tensor_tensor(out=St[h][:], in0=St[h][:], scalar=Eq[:, h, 127:128], in1=pkv[:, h * 256:(h + 1) * 256],
                                                             op0=ALU.mult, op1=ALU.add), reads=[St[h], Eq, pkv], writes=[St[h]])
            P.op("pool", lambda e, h=h: e.tensor_copy(out=Sb[h][:], in_=St[h][:]), reads=[St[h]], writes=[Sb[h]])
        y = yo[ci % 2]
        P.op("act", lambda e: e.activation(out=osb[:], in_=po[:, 0:VW].rearrange("p (h d) -> p h d", h=NH), func=AF.Copy), reads=[po], writes=[osb])
        P.op("pool", lambda e: e.tensor_tensor(out=sq[:], in0=osb[:], in1=osb[:], op=ALU.mult), reads=[osb], writes=[sq])
        P.op("dve", lambda e: e.tensor_reduce(out=rs[:, 0:NH], in_=sq[:], axis=AX.X, op=ALU.add), reads=[sq], writes=[rs])
        P.op("act", lambda e: e.activation(out=rs[:, 0:NH], in_=rs[:, 0:NH], func=AF.Ln, bias=eps5[:, 0:1], scale=1.0 / 256.0), reads=[rs, eps5], writes=[rs])
        P.op("act", lambda e: e.activation(out=rs[:, 4:4 + NH], in_=rs[:, 0:NH], func=AF.Exp, scale=-0.5), reads=[rs], writes=[rs])
        P.op("pool", lambda e, y=y: e.tensor_tensor(out=y[:], in0=osb[:], in1=rs[:, 4:4 + NH].unsqueeze(2).to_broadcast([128, NH, 256]), op=ALU.mult),
             reads=[osb, rs], writes=[y])
        P.op("pool", lambda e, y=y: e.tensor_tensor(out=y[:], in0=y[:], in1=ng[:].unsqueeze(1).to_broadcast([128, NH, 256]), op=ALU.mult),
             reads=[y, ng], writes=[y])
        y16 = yo16[ci % 2]
        P.op("dve", lambda e, y=y, y16=y16: e.tensor_tensor(out=y16[:], in0=y[:], in1=sr[:].rearrange("p (h d) -> p h d", h=NH), op=ALU.mult),
             reads=[y, sr], writes=[y16])
        P.dma("pool", o_d, o_d[ci * 128:(ci + 1) * 128, ocol:ocol + VW], y16, y16[:].rearrange("p h d -> p (h d)"), persistent=True)
    P.release(m0)


RW_C = 0.6065306597126334


class _Stop(Exception):
    pass


W_DBG = 0


def phase_W(P, S, NPR, h_d, mix_d, vecsT_d, wr_d, wk_d, wv_d, w1_d, a1_d, g1_d, w2_d, a2_d, g2_d, vecs_d, cst_d, o_d, ocol=0):
    try:
        _phase_W(P, S, NPR, h_d, mix_d, vecsT_d, wr_d, wk_d, wv_d, w1_d, a1_d, g1_d, w2_d, a2_d, g2_d, vecs_d, cst_d, o_d, ocol)
    except _Stop:
        P.barrier()
        P.top = 0


def _phase_W(P, S, NPR, h_d, mix_d, vecsT_d, wr_d, wk_d, wv_d, w1_d, a1_d, g1_d, w2_d, a2_d, g2_d, vecs_d, cst_d, o_d, ocol=0):
    m0 = P.mark()
    CH = NPR * 128
    ng = S // 512
    B = P.banks
    cst = P.sb([128, 6, 128], F32)
    P.dma("sp", cst, cst[:], cst_d, cst_d[0:6].rearrange("a p n -> p a n"))
    identf, MS, MSt, MI, bones, hsel = [cst[:, i, :] for i in range(6)]
    identb = P.sb([128, 128], BF16)
    P.dma("pool", identb, identb[:], cst_d, cst_d[0])
    rmask = P.sb([128, 512], F32)
    P.dma("sp", rmask, rmask[:], cst_d, cst_d[6, 0:4, :].rearrange("a n -> (a n)").partition_broadcast(128))
    vec = P.sb([128, 8, NPR], F32)
    P.dma("sp", vec, vec[:], vecsT_d, vecsT_d.rearrange("p (a c) -> p a c", a=8))
    omka = P.sb([128, NPR], F32)
    P.op("dve", lambda e: e.tensor_scalar(out=omka[:], in0=vec[:, 3, :], scalar1=-1.0, scalar2=1.0, op0=ALU.mult, op1=ALU.add), reads=[vec], writes=[omka])
    lng = P.sb([128, CH], F32)
    P.dma("sp", lng, lng[:], vecs_d, vecs_d[5].partition_broadcast(128))
    lnb = P.sb([128, CH], F32)
    P.dma("sp", lnb, lnb[:], vecs_d, vecs_d[6].partition_broadcast(128))
    mixT = P.sb([128, 6, 8], F32)
    P.dma("sp", mixT, mixT[:], mix_d, mix_d.rearrange("p (g c) -> p g c", g=6))
    omix = P.sb([128, 6, 8], F32)
    P.op("dve", lambda e: e.tensor_scalar(out=omix[:], in0=mixT[:], scalar1=-1.0, scalar2=1.0, op0=ALU.mult, op1=ALU.add), reads=[mixT], writes=[omix])
    NW = 3 * CH + 256
    Wc = P.sb([128, 16, NW], BF16)
    cols = [(wr_d, 0, CH, 0), (wk_d, CH, CH, 1), (wv_d, 2 * CH, CH, 2), (w1_d, 3 * CH, 64, 3), (a1_d, 3 * CH + 64, 64, 4), (g1_d, 3 * CH + 128, 128, 5)]
    ms = P.mark()
    stg = [P.sb([128, 8, 512], F32) for _ in range(2)]
    for wi, (wd, c0, n, gi) in enumerate(cols):
        st = stg[wi % 2]
        P.dma("sp", st, st[:, :, 0:n], wd, wd.rearrange("(c p) n -> p c n", p=128))
        for c in range(8):
            P.op("dve", lambda e, st=st, c=c, c0=c0, n=n, gi=gi: e.tensor_scalar(out=Wc[:, c, c0:c0 + n], in0=st[:, c, 0:n], scalar1=omix[:, gi, c:c + 1],
                                                                               scalar2=None, op0=ALU.mult), reads=[st, omix], writes=[Wc])
            P.op("pool", lambda e, st=st, c=c, c0=c0, n=n, gi=gi: e.tensor_scalar(out=Wc[:, 8 + c, c0:c0 + n], in0=st[:, c, 0:n], scalar1=mixT[:, gi, c:c + 1],
                                                                                scalar2=None, op0=ALU.mult), reads=[st, mixT], writes=[Wc])
    P.release(ms)
    w2 = P.sb([64, CH], BF16)
    P.dma("pool", w2, w2[:], w2_d, w2_d[:])
    a2 = P.sb([64, CH], BF16)
    P.dma("pool", a2, a2[:], a2_d, a2_d[:])
    g2 = P.sb([128, CH], BF16)
    P.dma("pool", g2, g2[:], g2_d, g2_d[:])
    Z = [P.sb([128, 128], F32) for _ in range(NPR)]
    Ucs = [P.sb([128, 128], F32) for _ in range(2)]
    Ktc = [P.sb([128, 128], F32) for _ in range(2)]
    for z in Ucs + Ktc:
        P.op("dve", lambda e, z=z: e.memset(z[:], 0.0), writes=[z])
    for z in Z:
        P.op("dve", lambda e, z=z: e.memset(z[:], 0.0), writes=[z])
    if W_DBG == 1:
        raise _Stop()
    hf = [P.sb([128, 4, D], BF16) for _ in range(2)]
    xT = P.sb([128, 8, 514], BF16)
    P.op("dve", lambda e: e.memset(xT[:, :, 0:2], 0.0), writes=[xT])
    wl = P.sb([64, 512], BF16)
    al = P.sb([64, 512], BF16)
    gl = P.sb([128, 512], BF16)
    F = lambda: P.sb([128, 512], F32)
    sw, aa, Lp, t0, t1, EL, EnL, ELx, Erm, kk, kp, ka = [F() for _ in range(12)]
    rhat, ahat, bhat, khat, nbt, kt, vT, rkr = [F() for _ in range(8)]
    gC = P.sb([128, 8], F32)
    ahat2, rhat2, bhat2 = [P.sb([128, 2, 512], F32) for _ in range(3)]
    for z_ in (ahat2, rhat2, bhat2):
        P.op("pool", lambda e, z_=z_: e.memset(z_[:], 0.0), writes=[z_])
    Vtok, nBt, Kt, At, PP, WW, PT, U, Ysb, Yc, sqy = [P.sb([128, 128], F32) for _ in range(11)]
    Fm = [[P.sb([128, 128], F32) for _ in range(2)] for _ in range(2)]
    X = [P.sb([128, 128], F32) for _ in range(2)]
    NakT, nMrbT, MrkT = [[P.sb([128, 128], F32) for _ in range(2)] for _ in range(3)]
    st8 = P.sb([128, 16], F32)
    bcs = P.sb([128, 2], F32)
    ob = [P.sb([128, 128], BF16) for _ in range(2)]

    def load(g):
        P.dma("sp", hf[g % 2], hf[g % 2][:], h_d, h_d[g * 512:(g + 1) * 512, :].rearrange("(j p) n -> p j n", p=128))

    def tr32(src_t, src_ap, dst_t, dst_ap, eng="act"):
        P.op("pe", lambda e: e.transpose(out=B[0][:, 0:128], in_=src_ap, identity=identf), reads=[src_t, cst], writes=[B[0]])
        if eng == "act":
            P.op("act", lambda e: e.activation(out=dst_ap, in_=B[0][:, 0:128], func=AF.Copy), reads=[B[0]], writes=[dst_t])
        else:
            P.op("dve", lambda e: e.tensor_copy(out=dst_ap, in_=B[0][:, 0:128]), reads=[B[0]], writes=[dst_t])

    def proj(pb, out_ap, c0, n, reads_extra=()):
        for c in range(16):
            rhs = xT[:, c, 1:513] if c < 8 else xT[:, c - 8, 0:512]
            P.op("pe", lambda e, c=c, rhs=rhs: e.matmul(out_ap, lhsT=Wc[:, c, c0:c0 + n], rhs=rhs, start=(c == 0), stop=(c == 15)),
                 reads=[Wc, xT], writes=[pb], inc=(c == 15))

    def V_(fn, r, w):
        P.op("dve", fn, reads=r, writes=w)

    def A_(fn, r, w):
        P.op("act", fn, reads=r, writes=w)

    load(0)
    for g in range(ng):
        if g + 1 < ng:
            load(g + 1)
        if g > 0:
            V_(lambda e: e.tensor_copy(out=xT[:, :, 0:1], in_=xT[:, :, 512:513]), [xT], [xT])
        for j in range(4):
            transpose_to(P, hf[g % 2], hf[g % 2][:, j, :], 8, identb, B[0], xT, xT[:, :, 1 + j * 128:1 + (j + 1) * 128])
        proj(B[6], B[6][0:64, :], 3 * CH, 64)
        A_(lambda e: e.activation(out=wl[:], in_=B[6][0:64, :], func=AF.Tanh), [B[6]], [wl])
        proj(B[6], B[6][0:64, :], 3 * CH + 64, 64)
        A_(lambda e: e.activation(out=al[:], in_=B[6][0:64, :], func=AF.Copy), [B[6]], [al])
        proj(B[6], B[6][:, :], 3 * CH + 128, 128)
        A_(lambda e: e.activation(out=gl[:], in_=B[6][:, :], func=AF.Sigmoid), [B[6]], [gl])
        if W_DBG == 2:
            raise _Stop()
        for oc in range(NPR):
            osl = slice(oc * 128, (oc + 1) * 128)
            vcol = lambda i, oc=oc: vec[:, i, oc:oc + 1]
            rP, kP, vP = B[1], B[2], B[3]
            proj(rP, rP[:], oc * 128, 128)
            proj(kP, kP[:], CH + oc * 128, 128)
            proj(vP, vP[:], 2 * CH + oc * 128, 128)
            P.op("pe", lambda e, osl=osl: e.matmul(B[4][:], lhsT=w2[:, osl], rhs=wl[:], start=True, stop=True), reads=[w2, wl], writes=[B[4]])
            A_(lambda e, vcol=vcol: e.activation(out=sw[:], in_=B[4][:], func=AF.Sigmoid, bias=vcol(0)), [B[4], vec], [sw])
            P.op("pe", lambda e, osl=osl: e.matmul(B[4][:], lhsT=a2[:, osl], rhs=al[:], start=True, stop=True), reads=[a2, al], writes=[B[4]])
            A_(lambda e, vcol=vcol: e.activation(out=aa[:], in_=B[4][:], func=AF.Sigmoid, bias=vcol(1)), [B[4], vec], [aa])
            A_(lambda e: e.activation(out=vT[:], in_=vP[:], func=AF.Copy), [vP], [vT])
            V_(lambda e: e.tensor_tensor_scan(out=Lp[:], data0=rmask[:], data1=sw[:], initial=0.0, op0=ALU.mult, op1=ALU.add), [rmask, sw], [Lp])
            V_(lambda e: e.tensor_tensor(out=t0[:], in0=Lp[:], in1=sw[:], op=ALU.subtract), [Lp, sw], [t0])
            Lp3 = Lp[:].rearrange("p (c t) -> p c t", c=8)
            V_(lambda e: e.tensor_tensor(out=t1[:].rearrange("p (c t) -> p c t", c=8), in0=Lp3[:, :, 63:64].to_broadcast([128, 8, 64]), in1=Lp3, op=ALU.subtract),
               [Lp], [t1])
            A_(lambda e: e.activation(out=EL[:], in_=Lp[:], func=AF.Exp, scale=-RW_C), [Lp], [EL])
            A_(lambda e: e.activation(out=EnL[:], in_=Lp[:], func=AF.Exp, scale=RW_C), [Lp], [EnL])
            A_(lambda e: e.activation(out=ELx[:], in_=t0[:], func=AF.Exp, scale=-RW_C), [t0], [ELx])
            A_(lambda e: e.activation(out=Erm[:], in_=t1[:], func=AF.Exp, scale=-RW_C), [t1], [Erm])
            V_(lambda e: e.tensor_copy(out=gC[:], in_=EL[:].rearrange("p (c t) -> p c t", c=8)[:, :, 63]), [EL], [gC])
            V_(lambda e, vcol=vcol: e.tensor_scalar(out=kk[:], in0=kP[:], scalar1=vcol(2), scalar2=None, op0=ALU.mult), [kP, vec], [kk])
            P.op("pool", lambda e: e.tensor_tensor(out=t0[:], in0=kk[:], in1=kk[:], op=ALU.mult), reads=[kk], writes=[t0])
            P.op("pe", lambda e: e.matmul(B[5][:], lhsT=bones, rhs=t0[:], start=True, stop=True), reads=[cst, t0], writes=[B[5]])
            A_(lambda e: e.activation(out=t1[:], in_=B[5][:], func=AF.Sqrt, bias=1e-30), [B[5]], [t1])
            V_(lambda e: e.reciprocal(out=t1[:], in_=t1[:]), [t1], [t1])
            V_(lambda e: e.tensor_tensor(out=kk[:], in0=kk[:], in1=t1[:], op=ALU.mult), [kk, t1], [kk])
            V_(lambda e, vcol=vcol, oc=oc: e.tensor_scalar(out=t0[:], in0=aa[:], scalar1=vcol(3), scalar2=omka[:, oc:oc + 1], op0=ALU.mult, op1=ALU.add),
               [aa, vec, omka], [t0])
            V_(lambda e: e.tensor_tensor(out=kp[:], in0=kP[:], in1=t0[:], op=ALU.mult), [kP, t0], [kp])
            V_(lambda e: e.tensor_tensor(out=rhat[:], in0=rP[:], in1=EL[:], op=ALU.mult), [rP, EL], [rhat])
            V_(lambda e, vcol=vcol: e.scalar_tensor_tensor(out=rkr[:], in0=rP[:], scalar=vcol(4), in1=kp[:], op0=ALU.mult, op1=ALU.mult), [rP, vec, kp], [rkr])
            P.op("pool", lambda e: e.tensor_tensor(out=ahat[:], in0=kk[:], in1=ELx[:], op=ALU.mult), reads=[kk, ELx], writes=[ahat])
            P.op("pool", lambda e: e.tensor_tensor(out=ka[:], in0=kk[:], in1=aa[:], op=ALU.mult), reads=[kk, aa], writes=[ka])
            P.op("pool", lambda e: e.tensor_tensor(out=bhat[:], in0=ka[:], in1=EnL[:], op=ALU.mult), reads=[ka, EnL], writes=[bhat])
            V_(lambda e: e.scalar_tensor_tensor(out=nbt[:], in0=ka[:], scalar=-1.0, in1=Erm[:], op0=ALU.mult, op1=ALU.mult), [ka, Erm], [nbt])
            P.op("pool", lambda e: e.tensor_tensor(out=khat[:], in0=kp[:], in1=EnL[:], op=ALU.mult), reads=[kp, EnL], writes=[khat])
            P.op("pool", lambda e: e.tensor_tensor(out=kt[:], in0=kp[:], in1=Erm[:], op=ALU.mult), reads=[kp, Erm], writes=[kt])
            for src, dst in ((ahat, ahat2), (rhat, rhat2), (bhat, bhat2)):
                A_(lambda e, src=src, dst=dst: e.activation(out=dst[0:64, 0, :], in_=src[0:64, :], func=AF.Copy), [src], [dst])
                A_(lambda e, src=src, dst=dst: e.activation(out=dst[64:128, 1, :], in_=src[64:128, :], func=AF.Copy), [src], [dst])
            if W_DBG == 3:
                raise _Stop()
            for blk in range(4):
                ts = slice(blk * 128, (blk + 1) * 128)
                tr32(vT, vT[:, ts], Vtok, Vtok[:])
                tr32(nbt, nbt[:, ts], nBt, nBt[:], "dve")
                P.op("pe", lambda e, ts=ts: e.transpose(out=B[0][:, 0:128], in_=kt[:, ts], identity=identf), reads=[kt, cst], writes=[B[0]])
                A_(lambda e: e.activation(out=Ktc[0][0:64, :], in_=B[0][0:64, 0:128], func=AF.Copy), [B[0]], [Ktc[0]])
                A_(lambda e: e.activation(out=Ktc[1][64:128, :], in_=B[0][64:128, 0:128], func=AF.Copy), [B[0]], [Ktc[1]])
                tr32(ahat, ahat[:, ts], At, At[:], "dve")
                if W_DBG == 4:
                    raise _Stop()
                G0, G1, G2 = B[1], B[2], B[4]

                def mmg(out_ap, l, r2, pb, inc, ts=ts):
                    P.op("pe", lambda e: e.matmul(out_ap, lhsT=l[:, ts], rhs=r2[:, :, ts], start=True, stop=True), reads=[l, r2], writes=[pb], inc=inc)
                mmg(G0[:, 0:256], bhat, ahat2, G0, False)
                mmg(G0[:, 256:512], ahat, bhat2, G0, True)
                mmg(G1[:, 0:256], khat, ahat2, G1, False)
                mmg(G1[:, 256:512], bhat, rhat2, G1, True)
                mmg(G2[:, 0:256], khat, rhat2, G2, True)
                for h in range(2):
                    hs = slice(h * 64, (h + 1) * 64)
                    c0_, c1_ = h * 128, 256 + h * 128
                    Fc, FTc = Fm[0]
                    V_(lambda e, Fc=Fc, c0_=c0_: e.scalar_tensor_tensor(out=Fc[:], in0=G0[:, c0_:c0_ + 128], scalar=-1.0, in1=MS, op0=ALU.mult, op1=ALU.mult), [G0, cst], [Fc])
                    V_(lambda e, FTc=FTc, c1_=c1_: e.scalar_tensor_tensor(out=FTc[:], in0=G0[:, c1_:c1_ + 128], scalar=-1.0, in1=MSt, op0=ALU.mult, op1=ALU.mult), [G0, cst], [FTc])
                    V_(lambda e, h=h, c0_=c0_: e.tensor_tensor(out=NakT[h][:], in0=G1[:, c0_:c0_ + 128], in1=MS, op=ALU.mult), [G1, cst], [NakT[h]])
                    V_(lambda e, h=h, c1_=c1_: e.scalar_tensor_tensor(out=nMrbT[h][:], in0=G1[:, c1_:c1_ + 128], scalar=-1.0, in1=MI, op0=ALU.mult, op1=ALU.mult), [G1, cst], [nMrbT[h]])
                    V_(lambda e, h=h, c0_=c0_: e.tensor_tensor(out=MrkT[h][:], in0=G2[:, c0_:c0_ + 128], in1=MI, op=ALU.mult), [G2, cst], [MrkT[h]])
                    if W_DBG == 5:
                        raise _Stop()
                    P.op("pe", lambda e, h=h, hs=hs: e.matmul(B[7][:, 0:64], lhsT=NakT[h][:], rhs=Vtok[:, hs], start=True, stop=True),
                         reads=[NakT[h], Vtok], writes=[B[7]])
                    A_(lambda e, hs=hs: e.activation(out=X[0][:, 0:64], in_=At[:, hs], func=AF.Copy), [At], [X[0]])
                    A_(lambda e: e.activation(out=X[0][:, 64:128], in_=B[7][:, 0:64], func=AF.Copy), [B[7]], [X[0]])
                    for lv in range(6):
                        Fc, FTc = Fm[lv % 2]
                        Fn, FTn = Fm[(lv + 1) % 2]
                        Xc, Xn = X[lv % 2], X[(lv + 1) % 2]
                        P.op("pe", lambda e, Fc=Fc, Xc=Xc: e.matmul(B[3][:, 0:128], lhsT=Fc[:], rhs=Xc[:], start=True, stop=True), reads=[Fc, Xc], writes=[B[3]])
                        if lv < 5:
                            P.op("pe", lambda e, Fc=Fc, FTc=FTc: e.matmul(B[7][:, 0:128], lhsT=FTc[:], rhs=Fc[:], start=True, stop=True),
                                 reads=[Fc, FTc], writes=[B[7]], inc=False)
                            P.op("pe", lambda e, Fc=Fc, FTc=FTc: e.matmul(B[7][:, 128:256], lhsT=Fc[:], rhs=FTc[:], start=True, stop=True),
                                 reads=[Fc, FTc], writes=[B[7]])
                            V_(lambda e, Xc=Xc, Xn=Xn: e.tensor_tensor(out=Xn[:], in0=B[3][:, 0:128], in1=Xc[:], op=ALU.add), [B[3], Xc], [Xn])
                            A_(lambda e, Fn=Fn: e.activation(out=Fn[:], in_=B[7][:, 0:128], func=AF.Copy), [B[7]], [Fn])
                            A_(lambda e, FTn=FTn: e.activation(out=FTn[:], in_=B[7][:, 128:256], func=AF.Copy), [B[7]], [FTn])
                        else:
                            V_(lambda e, Xc=Xc, hs=hs: e.tensor_tensor(out=PP[:, hs], in0=B[3][:, 0:64], in1=Xc[:, 0:64], op=ALU.add), [B[3], Xc], [PP])
                            V_(lambda e, Xc=Xc, hs=hs: e.tensor_tensor(out=WW[:, hs], in0=B[3][:, 64:128], in1=Xc[:, 64:128], op=ALU.add), [B[3], Xc], [WW])
                if W_DBG == 6:
                    raise _Stop()
                tr32(PP, PP[:], PT, PT[:])
                z = Z[oc]
                pU, pY, pZ = B[4], B[5], B[6]
                for cp in range(2):
                    rt = slice(cp * 64, (cp + 1) * 64)
                    Uc = Ucs[cp]
                    P.op("pe", lambda e, z=z: e.matmul(pU[:, 0:128], lhsT=PT[:], rhs=z[:], start=True, stop=True), reads=[PT, z], writes=[pU])
                    V_(lambda e, rt=rt, Uc=Uc: e.tensor_tensor(out=Uc[rt, :], in0=pU[rt, 0:128], in1=WW[rt, :], op=ALU.add), [pU, WW], [Uc])
                    P.op("pe", lambda e, ts=ts, z=z: e.matmul(pY[:, 0:128], lhsT=rhat[:, ts], rhs=z[:], start=True, stop=False), reads=[rhat, z], writes=[pY], inc=False)
                    for h in range(2):
                        hs = slice(h * 64, (h + 1) * 64)
                        P.op("pe", lambda e, hs=hs, h=h, Uc=Uc: e.matmul(pY[:, hs], lhsT=nMrbT[h][:], rhs=Uc[:, hs], start=False, stop=False),
                             reads=[nMrbT[h], Uc], writes=[pY], inc=False)
                        P.op("pe", lambda e, hs=hs, h=h: e.matmul(pY[:, hs], lhsT=MrkT[h][:], rhs=Vtok[:, hs], start=False, stop=(h == 1)),
                             reads=[MrkT[h], Vtok], writes=[pY], inc=(h == 1))
                    A_(lambda e, rt=rt: e.activation(out=Ysb[rt, :], in_=pY[rt, 0:128], func=AF.Copy), [pY], [Ysb])
                    for h in range(2):
                        hs = slice(h * 64, (h + 1) * 64)
                        P.op("pe", lambda e, hs=hs, Uc=Uc: e.matmul(pZ[:, hs], lhsT=nBt[:], rhs=Uc[:, hs], start=True, stop=False),
                             reads=[nBt, Uc], writes=[pZ], inc=False)
                        P.op("pe", lambda e, hs=hs, cp=cp: e.matmul(pZ[:, hs], lhsT=Ktc[cp][:], rhs=Vtok[:, hs], start=False, stop=True),
                             reads=[Ktc[cp], Vtok], writes=[pZ], inc=(h == 1))
                    ci = blk * 2 + cp
                    for h in range(2):
                        hs = slice(h * 64, (h + 1) * 64)
                        V_(lambda e, ci=ci, hs=hs, z=z: e.scalar_tensor_tensor(out=z[hs, hs], in0=z[hs, hs], scalar=gC[hs, ci:ci + 1], in1=pZ[hs, hs],
                                                                        op0=ALU.mult, op1=ALU.add), [z, gC, pZ], [z])
                if W_DBG == 7:
                    raise _Stop()
                P.op("pe", lambda e, ts=ts: e.matmul(B[7][:, 0:2], lhsT=rkr[:, ts], rhs=hsel[:, 0:2], start=True, stop=True), reads=[rkr, cst], writes=[B[7]], inc=False)
                P.op("pe", lambda e, ts=ts, osl=osl: e.matmul(B[7][:, 128:256], lhsT=gl[:, ts], rhs=g2[:, osl], start=True, stop=True), reads=[gl, g2], writes=[B[7]])
                Y3 = Ysb[:].rearrange("p (h v) -> p h v", h=2)
                Yc3 = Yc[:].rearrange("p (h v) -> p h v", h=2)
                V_(lambda e: e.tensor_reduce(out=st8[:, 0:2], in_=Y3, axis=AX.X, op=ALU.add), [Ysb], [st8])
                V_(lambda e: e.tensor_scalar(out=st8[:, 2:4], in0=st8[:, 0:2], scalar1=1.0 / 64.0, scalar2=None, op0=ALU.mult), [st8], [st8])
                V_(lambda e: e.tensor_tensor(out=Yc3, in0=Y3, in1=st8[:, 2:4].unsqueeze(2).to_broadcast([128, 2, 64]), op=ALU.subtract), [Ysb, st8], [Yc])
                P.op("pool", lambda e: e.tensor_tensor(out=sqy[:], in0=Yc[:], in1=Yc[:], op=ALU.mult), reads=[Yc], writes=[sqy])
                V_(lambda e: e.tensor_reduce(out=st8[:, 4:6], in_=sqy[:].rearrange("p (h v) -> p h v", h=2), axis=AX.X, op=ALU.add), [sqy], [st8])
                A_(lambda e: e.activation(out=st8[:, 6:8], in_=st8[:, 4:6], func=AF.Sqrt, bias=64e-5, scale=1.0 / 64.0), [st8], [st8])
                V_(lambda e: e.reciprocal(out=st8[:, 8:10], in_=st8[:, 6:8]), [st8], [st8])
                V_(lambda e: e.tensor_tensor(out=Yc3, in0=Yc3, in1=st8[:, 8:10].unsqueeze(2).to_broadcast([128, 2, 64]), op=ALU.mult), [Yc, st8], [Yc])
                P.op("pool", lambda e, osl=osl: e.tensor_tensor(out=Yc[:], in0=Yc[:], in1=lng[:, osl], op=ALU.mult), reads=[Yc, lng], writes=[Yc])
                P.op("pool", lambda e, osl=osl: e.tensor_tensor(out=Yc[:], in0=Yc[:], in1=lnb[:, osl], op=ALU.add), reads=[Yc, lnb], writes=[Yc])
                V_(lambda e: e.tensor_copy(out=bcs[:], in_=B[7][:, 0:2]), [B[7]], [bcs])
                P.op("pool", lambda e: e.tensor_tensor(out=sqy[:].rearrange("p (h v) -> p h v", h=2), in0=Vtok[:].rearrange("p (h v) -> p h v", h=2),
                                                       in1=bcs[:].unsqueeze(2).to_broadcast([128, 2, 64]), op=ALU.mult), reads=[Vtok, bcs], writes=[sqy])
                P.op("pool", lambda e: e.tensor_tensor(out=Yc[:], in0=Yc[:], in1=sqy[:], op=ALU.add), reads=[Yc, sqy], writes=[Yc])
                o = ob[(oc * 4 + blk) % 2]
                V_(lambda e, o=o: e.tensor_tensor(out=o[:], in0=Yc[:], in1=B[7][:, 128:256], op=ALU.mult), [Yc, B[7]], [o])
                r0 = g * 512 + blk * 128
                P.dma("pool", o_d, o_d[r0:r0 + 128, ocol + oc * 128:ocol + (oc + 1) * 128], o, o[:], persistent=True)
    P.release(m0)


def rwkv_consts():
    ar = np.arange(128)
    same = (ar[:, None] // 64) == (ar[None, :] // 64)
    c = np.zeros((8, 128, 128), np.float32)
    c[0] = np.eye(128)
    c[1] = same & (ar[:, None] < ar[None, :])
    c[2] = c[1].T
    c[3] = same & (ar[:, None] <= ar[None, :])
    c[4] = same
    c[5, :64, 0] = 1
    c[5, 64:, 1] = 1
    rm = np.ones(512, np.float32)
    rm[::64] = 0
    c[6, 0:4, :] = rm.reshape(4, 128)
    return c


def rwkv_vec_layouts(mix, vecs):
    mixT = np.ascontiguousarray(mix.reshape(6, 8, 128).transpose(2, 0, 1)).reshape(128, 48)
    ch = vecs.shape[1]
    vT = np.ascontiguousarray(vecs.reshape(8, ch // 128, 128).transpose(2, 0, 1)).reshape(128, 8 * (ch // 128))
    return mixT.astype(np.float32), vT.astype(np.float32)


def lam_init_of(i):
    return 0.8 - 0.6 * math.exp(-0.3 * i)


def layer_inputs(inp, i):
    kind, j = i % 3, i // 3
    d = {}
    pre = "L%d_" % i
    for nm, key in (("g1", "ln1_g"), ("b1", "ln1_b"), ("g2", "ln2_g"), ("b2", "ln2_b"), ("wup", "mlp_up"), ("wdn", "mlp_down"),
                    ("wg", "ple_gate"), ("wp", "ple_proj")):
        d[pre + nm] = np.ascontiguousarray(inp[key][i])
    if kind == 0:
        w = inp["da_w_qkv"][j]
        for hh in range(2):
            sl = slice(hh * 512, (hh + 1) * 512)
            d[pre + "w3_%d" % hh] = np.ascontiguousarray(np.concatenate([w[:, 0:1024][:, sl], w[:, 1024:2048][:, sl], w[:, 2048:3072][:, sl]], 1))
            heads = list(range(hh * 4, hh * 4 + 4))
            d[pre + "btab_%d" % hh] = attn_bias_tables(inp["rel_bias"], heads)
            d[pre + "b31_%d" % hh] = np.ascontiguousarray(inp["rel_bias"][31, heads])
        d[pre + "lam"] = np.stack([inp["da_lam_q1"][j], inp["da_lam_k1"][j], inp["da_lam_q2"][j], inp["da_lam_k2"][j]]).astype(np.float32)
        d[pre + "subg"] = np.ascontiguousarray(inp["da_subln_g"][j])
        d[pre + "wo"] = np.ascontiguousarray(inp["da_w_o"][j])
    elif kind == 1:
        w = inp["gla_w_in"][j]
        for hh in range(2):
            d[pre + "w4_%d" % hh] = np.ascontiguousarray(np.concatenate(
                [w[:, hh * 256:(hh + 1) * 256], w[:, 512 + hh * 256:512 + (hh + 1) * 256],
                 w[:, 1024 + hh * 512:1024 + (hh + 1) * 512], w[:, 2048 + hh * 512:2048 + (hh + 1) * 512]], 1))
            d[pre + "wa2_%d" % hh] = np.ascontiguousarray(inp["gla_w_a2"][j][:, hh * 256:(hh + 1) * 256])
            d[pre + "ba_%d" % hh] = np.ascontiguousarray(inp["gla_b_a"][j][hh * 256:(hh + 1) * 256])
        d[pre + "wa1"] = np.ascontiguousarray(inp["gla_w_a1"][j])
        d[pre + "ng"] = np.ascontiguousarray(inp["gla_norm_g"][j])
        d[pre + "wo"] = np.ascontiguousarray(inp["gla_w_o"][j])
    else:
        for hh in range(2):
            sl = slice(hh * 512, (hh + 1) * 512)
            for nm, a in (("wr", inp["rw_w_rkv"][j][0]), ("wk", inp["rw_w_rkv"][j][1]), ("wv", inp["rw_w_rkv"][j][2]),
                          ("w2", inp["rw_w2"][j]), ("a2", inp["rw_a2"][j]), ("g2", inp["rw_g2"][j])):
                d[pre + nm + "_%d" % hh] = np.ascontiguousarray(a[:, sl])
            vecs = np.stack([inp["rw_w0"][j][sl], inp["rw_a0"][j][sl], inp["rw_k_k"][j][sl], inp["rw_k_a"][j][sl],
                             inp["rw_r_k"][j].reshape(-1)[sl], inp["rw_ln_g"][j][sl], inp["rw_ln_b"][j][sl],
                             inp["rw_ln_b"][j][sl]]).astype(np.float32)
            mixT, vT = rwkv_vec_layouts(inp["rw_mix"][j], vecs)
            d[pre + "vecs_%d" % hh] = vecs
            d[pre + "vecsT_%d" % hh] = vT
            d[pre + "mixT"] = mixT
        d[pre + "rw1"] = np.ascontiguousarray(inp["rw_w1"][j])
        d[pre + "ra1"] = np.ascontiguousarray(inp["rw_a1"][j])
        d[pre + "rg1"] = np.ascontiguousarray(inp["rw_g1"][j])
        d[pre + "wo"] = np.ascontiguousarray(inp["rw_w_o"][j])
    return d


def const_inputs():
    ar = np.arange(128)
    return {"ident": np.eye(128, dtype=np.float32),
            "mask": (ar[None, :] >= ar[:, None]).astype(np.float32),
            "tris": (ar[:, None] > ar[None, :]).astype(np.float32),
            "rwcst": rwkv_consts()}


PHASE_SEL = ("mix", "Ra", "Rb", "Rc")


def build_program(layers, shapes, S=SEQ):
    nc = bass.Bass("TRN2", target_bir_lowering=False)
    P = Prog(nc)
    IN = {}
    for name, shp in shapes.items():
        IN[name] = P.dram(name, list(shp), F32, kind="ExternalInput")
    out = P.dram("out", [S, D], F32, kind="ExternalOutput")
    o_d = P.dram("scr_o", [S, D], BF16)
    h1_d = P.dram("scr_h1", [S, D], F32)
    h1b_d = P.dram("scr_h1b", [S, D], BF16)
    h2_d = P.dram("scr_h2", [S, D], F32)
    h2b_d = P.dram("scr_h2b", [S, D], BF16)
    hbuf = [P.dram("scr_ha", [S, D], F32), P.dram("scr_hb", [S, D], F32)]
    hbb = [P.dram("scr_hab", [S, D], BF16), P.dram("scr_hbb", [S, D], BF16), P.dram("scr_xb", [S, D], BF16)]
    h_in = IN["x"]
    hb_in = hbb[2]
    phase_X(P, S, h_in, hb_in)
    for n, i in enumerate(layers):
        kind = i % 3
        pre = "L%d_" % i
        g = lambda nm: IN[pre + nm]
        for hh in range(2):
            if "mix" not in PHASE_SEL:
                break
            if kind == 0:
                phase_A(P, S, 4, lam_init_of(i), hb_in, g("w3_%d" % hh), g("btab_%d" % hh), IN["mask"], g("b31_%d" % hh), g("lam"), g("subg"),
                        IN["ident"], o_d, ocol=hh * 512)
            elif kind == 1:
                phase_G(P, S, 2, hb_in, g("w4_%d" % hh), g("wa1"), g("wa2_%d" % hh), g("ba_%d" % hh), g("ng"), IN["ident"], IN["mask"], IN["tris"],
                        o_d, ocol=hh * 512)
            else:
                phase_W(P, S, 4, hb_in, g("mixT"), g("vecsT_%d" % hh), g("wr_%d" % hh), g("wk_%d" % hh), g("wv_%d" % hh), g("rw1"), g("ra1"), g("rg1"),
                        g("w2_%d" % hh), g("a2_%d" % hh), g("g2_%d" % hh), g("vecs_%d" % hh), IN["rwcst"], o_d, ocol=hh * 512)
        prew = []
        if "Ra" in PHASE_SEL:
            def prefetch(g=g, prew=prew):
                th = []
                prew.append(load_weight_bf16(P, g("wup"), g("wup")[:], D, DFF, high=True, deferred=th))
                prew.append(load_weight_bf16(P, g("wdn"), g("wdn")[:], DFF, D, high=True, deferred=th))
                return th
            phase_Ra(P, S, o_d, h_in, g("wo"), g("g1"), g("b1"), IN["ident"], h1_d, h1b_d,
                     prefetch=prefetch if "Rb" in PHASE_SEL else None)
        if "Rb" in PHASE_SEL:
            phase_Rb(P, S, h1_d, h1b_d, g("wup"), g("wdn"), g("g2"), g("b2"), IN["ident"], h2_d, h2b_d, pre=tuple(prew) if prew else None)
            P.free_high()
        h_out = out if n == len(layers) - 1 else hbuf[n % 2]
        hb_out = hbb[n % 2]
        if "Rc" in PHASE_SEL:
            phase_Rc(P, S, h2_d, h2b_d, IN["p%d" % i], g("wg"), g("wp"), IN["ident"], h_out, hb_out)
        h_in, hb_in = h_out, hb_out
    P.finish()
    return nc


LAYER_GROUPS = [[0, 1, 2, 3]]


def kernel(**inputs):
    inp = {k: np.asarray(v, dtype=np.float32) for k, v in inputs.items()}
    x = inp["x"]
    h = [np.ascontiguousarray(x[b]) for b in range(BATCH)]
    consts = const_inputs()
    for layers in LAYER_GROUPS:
        shared = dict(consts)
        for i in layers:
            shared.update(layer_inputs(inp, i))
        in_maps = []
        for c in range(8):
            b = c % BATCH
            m = dict(shared)
            m["x"] = h[b]
            for i in layers:
                m["p%d" % i] = np.ascontiguousarray(inp["p"][i, b])
            in_maps.append(m)
        shapes = {k: v.shape for k, v in in_maps[0].items()}
        nc = build_program(layers, shapes)
        res = run_bass_kernel_spmd(nc, in_maps, core_ids=list(range(8)))
        h = [np.asarray(res.results[b]["out"], dtype=np.float32) for b in range(BATCH)]
    return np.stack(h, 0).astype(np.float32)
```

```python
import math
import numpy as np
from contextlib import ExitStack
import concourse.bass as bass
import concourse.mybir as mybir
from concourse.bass_utils import run_bass_kernel_spmd

F32 = mybir.dt.float32
BF16 = mybir.dt.bfloat16
AF = mybir.ActivationFunctionType
ALU = mybir.AluOpType
AX = mybir.AxisListType

ENGS = ["pe", "act", "dve", "pool", "sp"]
ARENA_F32 = 53000

D = 1024
DFF = 4096
DEPTH = 4
SEQ = 4096
BATCH = 4
PLE = 256
DEEP_ALPHA = (2.0 * DEPTH) ** 0.25
LN_EPS = 1e-5


class Tile:
    def __init__(self, h, name, space="sb"):
        self.h = h
        self.name = name
        self.space = space
        self.last_w = []
        self.reads = {}
        self.dsem = None
        self.ssem = None

    def __getitem__(self, idx):
        return self.h[idx]

    def bf(self):
        return self.h.bitcast(BF16)

    def __getattr__(self, a):
        return getattr(self.h, a)


class Prog:
    def __init__(self, nc):
        self.nc = nc
        self.es = ExitStack()
        self.ops = {e: [] for e in ENGS}
        self.cnt = {}
        self.seen = {e: {} for e in ENGS}
        self.semh = {}
        self.nsem = 0
        self.ntile = 0
        self.free_dsems = []
        self.scope_dsems = []
        for e in ENGS:
            self._sem(("eng", e))
        self.out_tiles = []
        self.arena = self.es.enter_context(nc.sbuf_tensor("arena", [128, ARENA_F32], F32))
        self.psum = self.es.enter_context(nc.psum_tensor("psum", [128, 4096], F32))
        self.top = 0
        self.banks = [Tile(self.psum[:, i * 512:(i + 1) * 512], "bank%d" % i) for i in range(8)]

    def _sem(self, key):
        if key not in self.semh:
            self.nsem += 1
            self.semh[key] = self.es.enter_context(self.nc.semaphore("s%d" % self.nsem))
            self.cnt[key] = 0
        return self.semh[key]

    def sb(self, shape, dtype=F32, name=None):
        self.ntile += 1
        name = name or ("t%d" % self.ntile)
        p = shape[0]
        n = int(np.prod(shape[1:]))
        words = n if dtype == F32 else (n + 1) // 2
        words = (words + 15) // 16 * 16
        off = self.top
        self.top += words
        assert self.top <= ARENA_F32, "SBUF arena overflow: %d" % self.top
        ap = self.arena[0:p, off:off + words]
        if dtype != F32:
            ap = ap.bitcast(dtype)
        ap = ap[:, 0:n]
        if len(shape) == 3:
            ap = ap.rearrange("p (a b) -> p a b", a=shape[1])
        elif len(shape) == 4:
            ap = ap.rearrange("p (a b c) -> p a b c", a=shape[1], b=shape[2])
        return Tile(ap, name)

    def dram(self, name, shape, dtype, kind="Internal"):
        t = self.nc.dram_tensor(name, list(shape), dtype, kind=kind)
        tl = Tile(t.ap(), name, space="dram")
        if kind == "ExternalOutput":
            self.out_tiles.append(tl)
        return tl

    def mark(self):
        return self.top

    def release(self, mark):
        self.barrier()
        self.top = mark
        self.free_dsems.extend(self.scope_dsems)
        self.scope_dsems = []

    def barrier(self):
        keys = list(self.cnt.keys())
        for e in ENGS:
            waits = []
            for k in keys:
                c = self.cnt[k]
                if c > 0 and self.seen[e].get(k, 0) < c:
                    self.seen[e][k] = c
                    waits.append((self._sem(k), c))

            def emit(eng, waits=waits):
                for s, c in waits:
                    eng.wait_ge(s, c)
            self.ops[e].append(emit)
        for b in self.banks:
            b.last_w = []
            b.reads = {}

    def _need(self, eng, waits, dep):
        if dep is None:
            return
        key, c = dep
        if key == ("eng", "pe") and eng == "pe":
            return
        if self.seen[eng].get(key, 0) >= c:
            return
        waits[key] = max(waits.get(key, 0), c)

    def _deps(self, eng, reads, writes, dma_out=None):
        waits = {}
        for t in reads:
            for dep in t.last_w:
                self._need(eng, waits, dep)
        for t in writes:
            dma_only = all(k[0] == "dma" for k, _ in t.last_w)
            if not (dma_out is not None and t is dma_out and dma_only and not t.reads):
                for dep in t.last_w:
                    self._need(eng, waits, dep)
            for k, c in t.reads.items():
                self._need(eng, waits, (k, c))
        for k, c in waits.items():
            self.seen[eng][k] = c
        return [(self._sem(k), c) for k, c in waits.items()]

    def op(self, eng, fn, reads=(), writes=(), inc=True):
        waits = self._deps(eng, reads, writes)
        key = ("eng", eng)
        if inc:
            self.cnt[key] += 1
            done = self.cnt[key]
        else:
            done = self.cnt[key] + 1
        sem = self._sem(key)

        def emit(e, fn=fn, waits=waits, inc=inc, sem=sem):
            for s, c in waits:
                e.wait_ge(s, c)
            ins = fn(e)
            if inc:
                ins.then_inc(sem, 1)
        self.ops[eng].append(emit)
        for t in reads:
            t.reads[key] = max(t.reads.get(key, 0), done)
        for t in writes:
            t.last_w = [(key, done)]
            t.reads = {}

    def _new_dsem(self, persistent):
        if self.free_dsems and not persistent:
            k = self.free_dsems.pop()
        else:
            k = ("dma", len(self.semh))
            self._sem(k)
        if not persistent:
            self.scope_dsems.append(k)
        return k

    def dma(self, q, out_t, out_ap, in_t, in_ap, persistent=False):
        is_store = in_t is not None and in_t.space == "sb" and out_t.space == "dram"
        if is_store:
            if in_t.ssem is None:
                in_t.ssem = self._new_dsem(False)
            key = in_t.ssem
        else:
            if out_t.dsem is None:
                out_t.dsem = self._new_dsem(persistent and out_t.space == "dram")
            key = out_t.dsem
        waits = self._deps(q, [in_t] if in_t is not None else [], [out_t], dma_out=out_t)
        self.cnt[key] += 16
        done = self.cnt[key]
        sem = self._sem(key)

        def emit(e, waits=waits, sem=sem, out_ap=out_ap, in_ap=in_ap):
            for s, c in waits:
                e.wait_ge(s, c)
            e.dma_start(out=out_ap, in_=in_ap).then_inc(sem, 16)
        self.ops[q].append(emit)
        if in_t is not None:
            in_t.reads[key] = max(in_t.reads.get(key, 0), done)
        if out_t.reads or not all(k[0] == "dma" for k, _ in out_t.last_w):
            out_t.last_w = [(key, done)]
        else:
            out_t.last_w = [(k, c) for k, c in out_t.last_w if k != key] + [(key, done)]
        out_t.reads = {}

    def allgather(self, out_t, in_t, groups):
        q = "pool"
        if out_t.dsem is None:
            out_t.dsem = ("dma", len(self.semh))
            self._sem(out_t.dsem)
        waits = self._deps(q, [in_t], [out_t])
        self.cnt[out_t.dsem] += 16
        done = self.cnt[out_t.dsem]
        sem = self._sem(out_t.dsem)

        def emit(e, waits=waits, sem=sem):
            for s_, c in waits:
                e.wait_ge(s_, c)
            e.collective_compute("AllGather", ALU.bypass, replica_groups=groups, ins=[in_t[:]], outs=[out_t[:]]).then_inc(sem, 16)
        self.ops[q].append(emit)
        in_t.reads[out_t.dsem] = max(in_t.reads.get(out_t.dsem, 0), done)
        out_t.last_w = [(out_t.dsem, done)]
        out_t.reads = {}

    def finish(self):
        self.barrier()
        with self.nc.Block() as block:
            for name, deco in (("sp", block.sync), ("pe", block.tensor), ("act", block.scalar),
                               ("dve", block.vector), ("pool", block.gpsimd)):
                lst = self.ops[name]

                def body(e, lst=lst):
                    for f in lst:
                        f(e)
                deco(body)
        self.es.close()


def load_weight_bf16(P, w_t, w_ap, K, N, name=None):
    kc = K // 128
    t = P.sb([128, kc, N], BF16, name)
    src = w_ap.rearrange("(c p) n -> p c n", p=128)
    step = max(1, 4096 // N)
    for c0 in range(0, kc, step):
        c1 = min(kc, c0 + step)
        P.dma("pool", t, t[:, c0:c1, :], w_t, src[:, c0:c1, :])
    return t


def load_bcast(P, v_t, v_ap, n, name=None):
    t = P.sb([128, n], F32, name)
    P.dma("sp", t, t[:], v_t, v_ap.partition_broadcast(128))
    return t


def layer_norm_tile(P, z, y, g_bc, b_bc, scr, eps=LN_EPS, n=1024, y16=None):
    nk = n // 512
    for k in range(nk):
        P.op("dve", lambda e, k=k: e.bn_stats(out=scr[:, 6 * k:6 * k + 6], in_=z[:, k * 512:(k + 1) * 512]),
             reads=[z], writes=[scr])
    P.op("dve", lambda e: e.bn_aggr(out=scr[:, 12:14], in_=scr[:, 0:6 * nk]), reads=[scr], writes=[scr])
    P.op("act", lambda e: e.activation(out=scr[:, 15:16], in_=scr[:, 13:14], func=AF.Sqrt, bias=eps), reads=[scr], writes=[scr])
    P.op("dve", lambda e: e.reciprocal(out=scr[:, 14:15], in_=scr[:, 15:16]), reads=[scr], writes=[scr])
    P.op("dve", lambda e: e.scalar_tensor_tensor(out=scr[:, 11:12], in0=scr[:, 12:13], scalar=-1.0, in1=scr[:, 14:15], op0=ALU.mult, op1=ALU.mult),
         reads=[scr], writes=[scr])
    P.op("act", lambda e: e.activation(out=y[:], in_=z[:], func=AF.Identity, scale=scr[:, 14:15], bias=scr[:, 11:12]), reads=[z, scr], writes=[y])
    P.op("pool", lambda e: e.tensor_tensor(out=y[:], in0=y[:], in1=g_bc[:], op=ALU.mult), reads=[y, g_bc], writes=[y])
    P.op("dve", lambda e: e.tensor_tensor(out=y[:], in0=y[:], in1=b_bc[:], op=ALU.add), reads=[y, b_bc], writes=[y])
    if y16 is not None:
        P.op("act", lambda e: e.activation(out=y16[:], in_=y[:], func=AF.Copy), reads=[y], writes=[y16])


def ln_part1(P, z, y, g_bc, scr, eps=LN_EPS, n=1024):
    nk = n // 512
    for k in range(nk):
        P.op("dve", lambda e, k=k: e.bn_stats(out=scr[:, 6 * k:6 * k + 6], in_=z[:, k * 512:(k + 1) * 512]), reads=[z], writes=[scr])
    P.op("dve", lambda e: e.bn_aggr(out=scr[:, 12:14], in_=scr[:, 0:6 * nk]), reads=[scr], writes=[scr])
    P.op("act", lambda e: e.activation(out=scr[:, 15:16], in_=scr[:, 13:14], func=AF.Sqrt, bias=eps), reads=[scr], writes=[scr])
    P.op("dve", lambda e: e.reciprocal(out=scr[:, 14:15], in_=scr[:, 15:16]), reads=[scr], writes=[scr])
    P.op("dve", lambda e: e.scalar_tensor_tensor(out=scr[:, 11:12], in0=scr[:, 12:13], scalar=-1.0, in1=scr[:, 14:15], op0=ALU.mult, op1=ALU.mult),
         reads=[scr], writes=[scr])
    P.op("act", lambda e: e.activation(out=y[:], in_=z[:], func=AF.Identity, scale=scr[:, 14:15], bias=scr[:, 11:12]), reads=[z, scr], writes=[y])
    P.op("pool", lambda e: e.tensor_tensor(out=y[:], in0=y[:], in1=g_bc[:], op=ALU.mult), reads=[y, g_bc], writes=[y])


def ln_part2(P, y, b_bc, y16):
    P.op("dve", lambda e: e.tensor_tensor(out=y[:], in0=y[:], in1=b_bc[:], op=ALU.add), reads=[y, b_bc], writes=[y])
    P.op("act", lambda e: e.activation(out=y16[:], in_=y[:], func=AF.Copy), reads=[y], writes=[y16])


def transpose_to(P, src_t, src_bf, nchunk, ident, pbank, dst_t, dst_ap, evac="act"):
    pv = pbank.bf()
    for c in range(nchunk):
        P.op("pe", lambda e, c=c: e.transpose(out=pv[:, c * 128:(c + 1) * 128], in_=src_bf[:, c * 128:(c + 1) * 128],
                                              identity=ident[:]),
             reads=[src_t, ident], writes=[pbank], inc=(c == nchunk - 1))
    src = pv[:, 0:nchunk * 128].rearrange("p (c t) -> p c t", c=nchunk)
    if evac == "act":
        P.op("act", lambda e: e.activation(out=dst_ap, in_=src, func=AF.Copy), reads=[pbank], writes=[dst_t])
    else:
        P.op("dve", lambda e: e.tensor_copy(out=dst_ap, in_=src), reads=[pbank], writes=[dst_t])


def phase_Ra(P, NT, o_d, h_d, wo_d, g_d, b_d, ident_d, out_d, outb_d):
    m = P.mark()
    nt = NT // 128
    ident = P.sb([128, 128], BF16)
    P.dma("pool", ident, ident[:], ident_d, ident_d[:])
    wo = load_weight_bf16(P, wo_d, wo_d[:], D, D)
    g_bc = load_bcast(P, g_d, g_d[:], D)
    b_bc = load_bcast(P, b_d, b_d[:], D)
    NB = 3
    h_f = [P.sb([128, D], F32) for _ in range(NB)]
    o_b = [P.sb([128, D], BF16) for _ in range(NB)]
    oT = [P.sb([128, 8, 128], BF16) for _ in range(NB)]
    z = [P.sb([128, D], F32) for _ in range(NB)]
    y = [P.sb([128, D], F32) for _ in range(NB)]
    y16 = [P.sb([128, D], BF16) for _ in range(NB)]
    scr = [P.sb([128, 16], F32) for _ in range(NB)]
    pT = [P.banks[0], P.banks[3]]
    pm = [[P.banks[1], P.banks[2]], [P.banks[4], P.banks[5]]]

    def load(t):
        i = t % NB
        P.dma("sp", o_b[i], o_b[i][:], o_d, o_d[t * 128:(t + 1) * 128, :])
        P.dma("sp", h_f[i], h_f[i][:], h_d, h_d[t * 128:(t + 1) * 128, :])

    def a0(t):
        i = t % NB
        transpose_to(P, o_b[i], o_b[i], 8, ident, pT[t % 2], oT[i], oT[i][:])

    def a1(t):
        i = t % NB
        for n in range(2):
            pmn = pm[t % 2][n]
            for c in range(8):
                P.op("pe", lambda e, n=n, c=c, i=i, pmn=pmn: e.matmul(pmn[:], lhsT=oT[i][:, c, :], rhs=wo[:, c, n * 512:(n + 1) * 512],
                                                                     start=(c == 0), stop=(c == 7)),
                     reads=[oT[i], wo], writes=[pmn], inc=(c == 7))
            P.op("dve", lambda e, n=n, i=i, pmn=pmn: e.scalar_tensor_tensor(out=z[i][:, n * 512:(n + 1) * 512], in0=h_f[i][:, n * 512:(n + 1) * 512],
                                                                           scalar=DEEP_ALPHA, in1=pmn[:], op0=ALU.mult, op1=ALU.add),
                 reads=[h_f[i], pmn], writes=[z[i]])

    def b1(t):
        i = t % NB
        ln_part1(P, z[i], y[i], g_bc, scr[i])

    def b2(t):
        i = t % NB
        ln_part2(P, y[i], b_bc, y16[i])
        P.dma("pool", out_d, out_d[t * 128:(t + 1) * 128, :], y[i], y[i][:], persistent=True)
        P.dma("pool", outb_d, outb_d[t * 128:(t + 1) * 128, :], y16[i], y16[i][:], persistent=True)

    load(0)
    if nt > 1:
        load(1)
    for k in range(-1, nt + 2):
        if 2 <= k + 2 < nt:
            load(k + 2)
        if 0 <= k + 1 < nt:
            a0(k + 1)
        if 0 <= k < nt:
            a1(k)
        if 0 <= k - 1 < nt:
            b1(k - 1)
        if 0 <= k - 2 < nt:
            b2(k - 2)
    P.release(m)


def phase_Rb(P, NT, h1_d, h1b_d, wup_d, wdn_d, g_d, b_d, ident_d, out_d, outb_d):
    m = P.mark()
    G = min(512, NT)
    ng = NT // G
    nj = G // 128
    ident = P.sb([128, 128], BF16)
    P.dma("pool", ident, ident[:], ident_d, ident_d[:])
    wup = load_weight_bf16(P, wup_d, wup_d[:], D, DFF)
    wdn = load_weight_bf16(P, wdn_d, wdn_d[:], DFF, D)
    g_bc = load_bcast(P, g_d, g_d[:], D)
    b_bc = load_bcast(P, b_d, b_d[:], D)
    ht = [P.sb([128, D], BF16) for _ in range(2)]
    hr = [P.sb([128, D], F32) for _ in range(2)]
    y16 = P.sb([128, D], BF16)
    h1T = P.sb([128, 8, G], BF16)
    aT = P.sb([128, 32, G], BF16)
    z = P.sb([128, D], F32)
    yy = P.sb([128, D], F32)
    scr = P.sb([128, 16], F32)
    relu_t = [P.sb([128, 512], F32) for _ in range(2)]
    pT = P.banks[0]
    pu = [P.banks[1], P.banks[2], P.banks[3]]
    pd = [P.banks[4], P.banks[5]]
    k = 0
    for g in range(ng):
        for j in range(nj):
            r0 = g * G + j * 128
            t = ht[k % 2]
            k += 1
            P.dma("sp", t, t[:], h1b_d, h1b_d[r0:r0 + 128, :])
            transpose_to(P, t, t, 8, ident, pT, h1T, h1T[:, :, j * 128:(j + 1) * 128])
        for f in range(32):
            pb = pu[f % 3]
            for c in range(8):
                P.op("pe", lambda e, f=f, c=c, pb=pb: e.matmul(pb[:, 0:G], lhsT=wup[:, c, f * 128:(f + 1) * 128], rhs=h1T[:, c, :],
                                                              start=(c == 0), stop=(c == 7)),
                     reads=[wup, h1T], writes=[pb], inc=(c == 7))
            rl = relu_t[f % 2]
            P.op("act", lambda e, pb=pb, rl=rl: e.activation(out=rl[:, 0:G], in_=pb[:, 0:G], func=AF.Relu), reads=[pb], writes=[rl])
            sq_eng = "pool" if f % 3 == 2 else "dve"
            P.op(sq_eng, lambda e, f=f, rl=rl: e.tensor_tensor(out=aT[:, f, :], in0=rl[:, 0:G], in1=rl[:, 0:G], op=ALU.mult),
                 reads=[rl], writes=[aT])
        for j in range(nj):
            r0 = g * G + j * 128
            hres = hr[j % 2]
            P.dma("sp", hres, hres[:], h1_d, h1_d[r0:r0 + 128, :])
            for n in range(2):
                for f in range(32):
                    P.op("pe", lambda e, n=n, f=f, j=j: e.matmul(pd[n][:], lhsT=aT[:, f, j * 128:(j + 1) * 128], rhs=wdn[:, f, n * 512:(n + 1) * 512],
                                                                start=(f == 0), stop=(f == 31)),
                         reads=[aT, wdn], writes=[pd[n]], inc=(f == 31))
                P.op("dve", lambda e, n=n, hres=hres: e.scalar_tensor_tensor(out=z[:, n * 512:(n + 1) * 512], in0=hres[:, n * 512:(n + 1) * 512],
                                                                           scalar=DEEP_ALPHA, in1=pd[n][:], op0=ALU.mult, op1=ALU.add),
                     reads=[hres, pd[n]], writes=[z])
            layer_norm_tile(P, z, yy, g_bc, b_bc, scr, y16=y16)
            P.dma("pool", out_d, out_d[r0:r0 + 128, :], yy, yy[:], persistent=True)
            P.dma("pool", outb_d, outb_d[r0:r0 + 128, :], y16, y16[:], persistent=True)
    P.release(m)


def phase_Rc(P, NT, h2_d, h2b_d, p_d, wg_d, wp_d, ident_d, out_d, outb_d):
    m = P.mark()
    nt = NT // 128
    ident = P.sb([128, 128], BF16)
    P.dma("pool", ident, ident[:], ident_d, ident_d[:])
    wg = load_weight_bf16(P, wg_d, wg_d[:], D, D)
    wp = load_weight_bf16(P, wp_d, wp_d[:], PLE, D)
    NB = 3
    NH_ = 5
    hf = [P.sb([128, D], F32) for _ in range(NH_)]
    pf = [P.sb([128, PLE], F32) for _ in range(NB)]
    hb = [P.sb([128, D], BF16) for _ in range(NB)]
    pb16 = [P.sb([128, PLE], BF16) for _ in range(NB)]
    hT = [P.sb([128, 10, 128], BF16) for _ in range(NB)]
    sg = [P.sb([128, D], F32) for _ in range(NB)]
    y = [P.sb([128, D], F32) for _ in range(NB)]
    y16 = [P.sb([128, D], BF16) for _ in range(NB)]
    ppv = [P.sb([128, D], F32) for _ in range(NB)]
    pT = [P.banks[0], P.banks[1]]
    pg = [[P.banks[2], P.banks[3]], [P.banks[4], P.banks[5]]]
    pp = [P.banks[6], P.banks[7]]

    def load(t):
        i = t % NB
        P.dma("sp", hf[t % NH_], hf[t % NH_][:], h2_d, h2_d[t * 128:(t + 1) * 128, :])
        P.dma("sp", hb[i], hb[i][:], h2b_d, h2b_d[t * 128:(t + 1) * 128, :])
        P.dma("sp", pf[i], pf[i][:], p_d, p_d[t * 128:(t + 1) * 128, :])

    def a0(t):
        i = t % NB
        P.op("act", lambda e, i=i: e.activation(out=pb16[i][:], in_=pf[i][:], func=AF.Copy), reads=[pf[i]], writes=[pb16[i]])
        transpose_to(P, hb[i], hb[i], 8, ident, pT[0], hT[i], hT[i][:, 0:8, :])
        transpose_to(P, pb16[i], pb16[i], 2, ident, pT[1], hT[i], hT[i][:, 8:10, :], evac="dve")

    def a1(t):
        i = t % NB
        for n in range(2):
            pgn, ppn = pg[t % 2][n], pp[n]
            for c in range(8):
                P.op("pe", lambda e, n=n, c=c, i=i, pgn=pgn: e.matmul(pgn[:], lhsT=hT[i][:, c, :], rhs=wg[:, c, n * 512:(n + 1) * 512],
                                                                     start=(c == 0), stop=(c == 7)),
                     reads=[hT[i], wg], writes=[pgn], inc=(c == 7))
            for c in range(2):
                P.op("pe", lambda e, n=n, c=c, i=i, ppn=ppn: e.matmul(ppn[:, 0:512], lhsT=hT[i][:, 8 + c, :], rhs=wp[:, c, n * 512:(n + 1) * 512],
                                                                     start=(c == 0), stop=(c == 1)),
                     reads=[hT[i], wp], writes=[ppn], inc=(c == 1))
            P.op("dve", lambda e, n=n, i=i, ppn=ppn: e.tensor_copy(out=ppv[i][:, n * 512:(n + 1) * 512], in_=ppn[:, 0:512]), reads=[ppn], writes=[ppv[i]])

    def b1(t):
        i = t % NB
        for n in range(2):
            pgn = pg[t % 2][n]
            P.op("act", lambda e, n=n, i=i, pgn=pgn: e.activation(out=sg[i][:, n * 512:(n + 1) * 512], in_=pgn[:], func=AF.Sigmoid),
                 reads=[pgn], writes=[sg[i]])
        P.op("pool", lambda e, i=i: e.tensor_tensor(out=sg[i][:], in0=sg[i][:], in1=ppv[i][:], op=ALU.mult), reads=[sg[i], ppv[i]], writes=[sg[i]])

    def b2(t):
        i = t % NB
        hft = hf[t % NH_]
        P.op("dve", lambda e, i=i, hft=hft: e.tensor_tensor(out=y[i][:], in0=sg[i][:], in1=hft[:], op=ALU.add),
             reads=[sg[i], hft], writes=[y[i]])
        P.op("act", lambda e, i=i: e.activation(out=y16[i][:], in_=y[i][:], func=AF.Copy), reads=[y[i]], writes=[y16[i]])
        P.dma("pool", out_d, out_d[t * 128:(t + 1) * 128, :], y[i], y[i][:], persistent=True)
        P.dma("pool", outb_d, outb_d[t * 128:(t + 1) * 128, :], y16[i], y16[i][:], persistent=True)

    load(0)
    if nt > 1:
        load(1)
    for k in range(-1, nt + 2):
        if 2 <= k + 2 < nt:
            load(k + 2)
        if 0 <= k + 1 < nt:
            a0(k + 1)
        if 0 <= k < nt:
            a1(k)
        if 0 <= k - 1 < nt:
            b1(k - 1)
        if 0 <= k - 2 < nt:
            b2(k - 2)
    P.release(m)


def phase_X(P, S, x_d, xb_d):
    m = P.mark()
    xf = [P.sb([128, 4, D], F32) for _ in range(2)]
    xb = [P.sb([128, 4, D], BF16) for _ in range(2)]
    for g in range(S // 512):
        i = g % 2
        P.dma("sp", xf[i], xf[i][:], x_d, x_d[g * 512:(g + 1) * 512, :].rearrange("(j p) n -> p j n", p=128))
        if g % 2 == 0:
            P.op("act", lambda e, i=i: e.activation(out=xb[i][:], in_=xf[i][:], func=AF.Copy), reads=[xf[i]], writes=[xb[i]])
        else:
            P.op("dve", lambda e, i=i: e.tensor_copy(out=xb[i][:], in_=xf[i][:]), reads=[xf[i]], writes=[xb[i]])
        P.dma("pool", xb_d, xb_d[g * 512:(g + 1) * 512, :].rearrange("(j p) n -> p j n", p=128), xb[i], xb[i][:], persistent=True)
    P.release(m)


def phase_A(P, S, NH, lam_init, h_d, w3_d, btab_d, mask_d, b31_d, lam_d, subg_d, ident_d, o_d, ocol=0):
    m0 = P.mark()
    HW = NH * 128
    nqt = S // 512
    nblk = S // 128
    ident = P.sb([128, 128], BF16)
    P.dma("pool", ident, ident[:], ident_d, ident_d[:])
    w3 = load_weight_bf16(P, w3_d, w3_d[:], D, 3 * HW)
    qT = P.sb([128, NH, S], BF16)
    kTz = [P.sb([128, NH, S], BF16) for _ in range(2)]
    P.op("pool", lambda e: e.memset(kTz[0][64:128, :, :], 0.0), writes=[kTz[0]])
    P.op("pool", lambda e: e.memset(kTz[1][0:64, :, :], 0.0), writes=[kTz[1]])
    V = P.sb([128, nblk, NH, 130], BF16)
    P.op("pool", lambda e: e.memset(V[:, :, :, 128:130], 1.0), writes=[V])

    bt = P.sb([128, NH, 2, 128], F32)
    P.dma("sp", bt, bt[:], btab_d, btab_d.rearrange("h t k q -> k h t q"))
    mask = P.sb([128, 128], F32)
    P.dma("sp", mask, mask[:], mask_d, mask_d[:])
    mneg = P.sb([128, 128], F32)
    P.op("dve", lambda e: e.tensor_scalar(out=mneg[:], in0=mask[:], scalar1=30000.0, scalar2=-30000.0, op0=ALU.mult, op1=ALU.add),
         reads=[mask], writes=[mneg])
    for hd in range(NH):
        P.op("dve", lambda e, hd=hd: e.tensor_tensor(out=bt[:, hd, 0, :], in0=bt[:, hd, 0, :], in1=mask[:], op=ALU.mult),
             reads=[bt, mask], writes=[bt])
        P.op("dve", lambda e, hd=hd: e.tensor_tensor(out=bt[:, hd, 0, :], in0=bt[:, hd, 0, :], in1=mneg[:], op=ALU.add),
             reads=[bt, mneg], writes=[bt])
    b31 = load_bcast(P, b31_d, b31_d[:], NH)
    lamv = P.sb([128, 4, 64], F32)
    P.dma("sp", lamv, lamv[:], lam_d, lam_d.rearrange("a d -> (a d)").partition_broadcast(128).rearrange("p (a d) -> p a d", a=4))
    lsc = P.sb([128, 8], F32)
    ltmp = P.sb([128, 2, 64], F32)
    P.op("dve", lambda e: e.tensor_tensor(out=ltmp[:, 0, :], in0=lamv[:, 0, :], in1=lamv[:, 1, :], op=ALU.mult), reads=[lamv], writes=[ltmp])
    P.op("dve", lambda e: e.tensor_tensor(out=ltmp[:, 1, :], in0=lamv[:, 2, :], in1=lamv[:, 3, :], op=ALU.mult), reads=[lamv], writes=[ltmp])
    P.op("dve", lambda e: e.tensor_reduce(out=lsc[:, 0:2], in_=ltmp[:], axis=AX.X, op=ALU.add), reads=[ltmp], writes=[lsc])
    P.op("act", lambda e: e.activation(out=lsc[:, 2:4], in_=lsc[:, 0:2], func=AF.Exp), reads=[lsc], writes=[lsc])
    P.op("dve", lambda e: e.tensor_tensor(out=lsc[:, 4:5], in0=lsc[:, 3:4], in1=lsc[:, 2:3], op=ALU.subtract), reads=[lsc], writes=[lsc])
    P.op("dve", lambda e: e.tensor_scalar(out=lsc[:, 5:6], in0=lsc[:, 4:5], scalar1=-lam_init, scalar2=None, op0=ALU.add), reads=[lsc], writes=[lsc])
    subg = load_bcast(P, subg_d, subg_d[:], 128)
    P.op("dve", lambda e: e.tensor_scalar(out=subg[:], in0=subg[:], scalar1=(1.0 - lam_init), scalar2=None, op0=ALU.mult), reads=[subg], writes=[subg])

    m1 = P.mark()
    hf = [P.sb([128, 4, D], BF16) for _ in range(2)]
    hT = P.sb([128, 8, 512], BF16)
    pT = P.banks[0]
    pq = [P.banks[1], P.banks[2], P.banks[3]]

    def load(g):
        P.dma("sp", hf[g % 2], hf[g % 2][:], h_d, h_d[g * 512:(g + 1) * 512, :].rearrange("(j p) n -> p j n", p=128))

    load(0)
    it = 0
    for g in range(nqt):
        if g + 1 < nqt:
            load(g + 1)
        for j in range(4):
            transpose_to(P, hf[g % 2], hf[g % 2][:, j, :], 8, ident, pT, hT, hT[:, :, j * 128:(j + 1) * 128])
        for hd in range(NH):
            for which in range(2):
                pb = pq[it % 3]
                it += 1
                col = which * HW + hd * 128
                for c in range(8):
                    P.op("pe", lambda e, c=c, col=col, pb=pb: e.matmul(pb[:], lhsT=w3[:, c, col:col + 128], rhs=hT[:, c, :],
                                                                      start=(c == 0), stop=(c == 7)),
                         reads=[w3, hT], writes=[pb], inc=(c == 7))
                if which == 0:
                    P.op("act", lambda e, hd=hd, g=g, pb=pb: e.activation(out=qT[:, hd, g * 512:(g + 1) * 512], in_=pb[:], func=AF.Copy, scale=0.125),
                         reads=[pb], writes=[qT])
                else:
                    P.op("dve", lambda e, hd=hd, g=g, pb=pb: e.tensor_copy(out=kTz[0][0:64, hd, g * 512:(g + 1) * 512], in_=pb[0:64, :]),
                         reads=[pb], writes=[kTz[0]])
                    P.op("dve", lambda e, hd=hd, g=g, pb=pb: e.tensor_copy(out=kTz[1][64:128, hd, g * 512:(g + 1) * 512], in_=pb[64:128, :]),
                         reads=[pb], writes=[kTz[1]])
        for j in range(4):
            pb = pq[it % 3]
            it += 1
            for c in range(8):
                P.op("pe", lambda e, c=c, j=j, pb=pb: e.matmul(pb[:, 0:HW], lhsT=hT[:, c, j * 128:(j + 1) * 128], rhs=w3[:, c, 2 * HW:3 * HW],
                                                              start=(c == 0), stop=(c == 7)),
                     reads=[w3, hT], writes=[pb], inc=(c == 7))
            eng = "dve" if j % 2 == 0 else "act"
            src = pb[:, 0:HW].rearrange("p (h d) -> p h d", h=NH)
            if eng == "dve":
                P.op("dve", lambda e, g=g, j=j, src=src: e.tensor_copy(out=V[:, g * 4 + j, :, 0:128], in_=src), reads=[pb], writes=[V])
            else:
                P.op("act", lambda e, g=g, j=j, src=src: e.activation(out=V[:, g * 4 + j, :, 0:128], in_=src, func=AF.Copy), reads=[pb], writes=[V])
    P.release(m1)

    pss = [P.banks[0], P.banks[1], P.banks[6], P.banks[7]]
    po = [P.banks[2], P.banks[3], P.banks[4], P.banks[5]]
    pts = [P.sb([128, 512], BF16) for _ in range(5)]
    tmpf = [P.sb([128, 128], F32) for _ in range(3)]
    oacc = [P.sb([128, 4, 132], F32) for _ in range(2)]
    omu = [[P.sb([128, 4, 128], F32) for _ in range(2)] for _ in range(2)]
    rec = P.sb([128, 8], F32)
    pending = []
    ob = [P.sb([128, 4, 128], F32) for _ in range(2)]
    ob16 = [P.sb([128, 4, 128], BF16) for _ in range(2)]
    sq = P.sb([128, 4, 128], F32)
    rs = P.sb([128, 8], F32)
    eps5 = P.sb([128, 1], F32)
    P.op("dve", lambda e: e.memset(eps5[:], 1e-5), writes=[eps5])
    items = [(hd, qt, m, kb) for hd in range(NH) for qt in range(nqt) for m in range(2) for kb in range(4 * qt + 4)]
    AHEAD = 3
    cnt = {"nt": 0, "ne": 0}

    def qk(i):
        hd, qt, m, kb = items[i]
        c0 = max(0, kb - 4 * qt)
        ps = pss[i % 4]
        P.op("pe", lambda e: e.matmul(ps[:, c0 * 128:512], lhsT=kTz[m][:, hd, kb * 128:(kb + 1) * 128],
                                      rhs=qT[:, hd, qt * 512 + c0 * 128:(qt + 1) * 512], start=True, stop=True),
             reads=[kTz[m], qT], writes=[ps])

    def expv(i):
        hd, qt, m, kb = items[i]
        c0 = max(0, kb - 4 * qt)
        ps = pss[i % 4]
        pt = pts[i % 5]
        cfar = max(c0, kb + 2 - 4 * qt)
        for c in range(c0, min(cfar, 4)):
            dlt = 4 * qt + c - kb
            tf = tmpf[cnt["nt"] % 3]
            cnt["nt"] += 1
            P.op("dve", lambda e, c=c, tf=tf, dlt=dlt: e.tensor_tensor(out=tf[:], in0=ps[:, c * 128:(c + 1) * 128], in1=bt[:, hd, dlt, :], op=ALU.add),
                 reads=[ps, bt], writes=[tf])
            P.op("act", lambda e, c=c, tf=tf: e.activation(out=pt[:, c * 128:(c + 1) * 128], in_=tf[:], func=AF.Exp), reads=[tf], writes=[pt])
        if cfar < 4:
            P.op("act", lambda e: e.activation(out=pt[:, cfar * 128:512], in_=ps[:, cfar * 128:512], func=AF.Exp, bias=b31[:, hd:hd + 1]),
                 reads=[ps, b31], writes=[pt])
        for c in range(c0, 4):
            P.op("pe", lambda e, c=c: e.matmul(po[c][:, 0:129], lhsT=pt[:, c * 128:(c + 1) * 128], rhs=V[:, kb, hd, 0:129],
                                               start=(kb == 0), stop=(kb == 4 * qt + c)),
                 reads=[pt, V], writes=[po[c]], inc=(c == 3))
        if kb != 4 * qt + 3:
            return
        oa = oacc[cnt["ne"] % 2]
        cnt["ne"] += 1
        for c in range(4):
            P.op("dve", lambda e, c=c: e.tensor_copy(out=oa[:, c, 0:129], in_=po[c][:, 0:129]), reads=[po[c]], writes=[oa])
        P.op("dve", lambda e: e.reciprocal(out=rec[:, 4 * m:4 * m + 4], in_=oa[:, :, 128]), reads=[oa], writes=[rec])
        u = hd * nqt + qt
        om = omu[u % 2]
        P.op("pool", lambda e: e.tensor_tensor(out=om[m][:], in0=oa[:, :, 0:128], in1=rec[:, 4 * m:4 * m + 4].unsqueeze(2).to_broadcast([128, 4, 128]), op=ALU.mult),
             reads=[oa, rec], writes=[om[m]])
        if m == 0:
            return

        def tail(hd=hd, qt=qt, om=om, u=u):
            o = ob[u % 2]
            o16 = ob16[u % 2]
            P.op("dve", lambda e: e.scalar_tensor_tensor(out=o[:], in0=om[1][:], scalar=lsc[:, 5:6], in1=om[0][:], op0=ALU.mult, op1=ALU.add),
                 reads=[om[0], om[1], lsc], writes=[o])
            P.op("pool", lambda e: e.tensor_tensor(out=sq[:], in0=o[:], in1=o[:], op=ALU.mult), reads=[o], writes=[sq])
            P.op("dve", lambda e: e.tensor_reduce(out=rs[:, 0:4], in_=sq[:], axis=AX.X, op=ALU.add), reads=[sq], writes=[rs])
            P.op("act", lambda e: e.activation(out=rs[:, 0:4], in_=rs[:, 0:4], func=AF.Ln, bias=eps5[:, 0:1], scale=1.0 / 128.0), reads=[rs, eps5], writes=[rs])
            P.op("act", lambda e: e.activation(out=rs[:, 4:8], in_=rs[:, 0:4], func=AF.Exp, scale=-0.5), reads=[rs], writes=[rs])
            P.op("pool", lambda e: e.tensor_tensor(out=o[:], in0=o[:], in1=rs[:, 4:8].unsqueeze(2).to_broadcast([128, 4, 128]), op=ALU.mult),
                 reads=[o, rs], writes=[o])
            P.op("pool", lambda e: e.tensor_tensor(out=o16[:], in0=o[:], in1=subg[:].unsqueeze(1).to_broadcast([128, 4, 128]), op=ALU.mult),
                 reads=[o, subg], writes=[o16])
            P.dma("pool", o_d, o_d[qt * 512:(qt + 1) * 512, ocol + hd * 128:ocol + (hd + 1) * 128].rearrange("(c p) d -> p c d", p=128), o16, o16[:], persistent=True)
        pending.append([3, tail])

    for i in range(len(items) + AHEAD):
        if i < len(items):
            qk(i)
        if i >= AHEAD:
            if pending:
                pending[0][0] -= 1
                if pending[0][0] <= 0:
                    pending.pop(0)[1]()
            expv(i - AHEAD)
    while pending:
        pending.pop(0)[1]()
    P.release(m0)


def t5_bucket_np(rel):
    n = np.maximum(rel, 0)
    nf = np.maximum(n, 1).astype(np.float32)
    large = 16 + (np.log(nf / np.float32(16)) / np.float32(math.log(128 / 16)) * np.float32(16)).astype(np.int32)
    large = np.minimum(large, 31)
    return np.where(n < 16, n, large)


def attn_bias_tables(rel_bias, heads):
    k = np.arange(128)[:, None]
    q = np.arange(128)[None, :]
    idx = np.stack([t5_bucket_np(q - k), t5_bucket_np(128 + q - k)], 0)
    tab = rel_bias[idx]
    return np.ascontiguousarray(np.transpose(tab[..., heads], (3, 0, 1, 2))).astype(np.float32)


def phase_G(P, S, NH, h_d, w4_d, wa1_d, wa2_d, ba_d, ng_d, ident_d, mask_d, tris_d, o_d, ocol=0):
    m0 = P.mark()
    QW = NH * 128
    VW = NH * 256
    nch = S // 128
    ident = P.sb([128, 128], BF16)
    P.dma("pool", ident, ident[:], ident_d, ident_d[:])
    mask = P.sb([128, 128], F32)
    P.dma("sp", mask, mask[:], mask_d, mask_d[:])
    trii = P.sb([128, 128], F32)
    tris = P.sb([128, 128], F32)
    P.dma("sp", tris, tris[:], tris_d, tris_d[:])
    P.op("dve", lambda e: e.tensor_scalar(out=trii[:], in0=mask[:], scalar1=-1.0 / 16.0, scalar2=None, op0=ALU.mult), reads=[mask], writes=[trii])
    P.op("dve", lambda e: e.tensor_scalar(out=tris[:], in0=tris[:], scalar1=-1.0 / 16.0, scalar2=None, op0=ALU.mult), reads=[tris], writes=[tris])
    w4 = load_weight_bf16(P, w4_d, w4_d[:], D, 2 * QW + 2 * VW)
    wa1 = load_weight_bf16(P, wa1_d, wa1_d[:], D, 16)
    wa2 = P.sb([16, QW], BF16)
    P.dma("pool", wa2, wa2[:], wa2_d, wa2_d[:])
    ba = P.sb([1, QW], BF16)
    P.dma("pool", ba, ba[:], ba_d, ba_d.rearrange("(o n) -> o n", o=1))
    ones = P.sb([1, 128], BF16)
    P.op("dve", lambda e: e.memset(ones[:], 1.0), writes=[ones])
    ng = load_bcast(P, ng_d, ng_d[:], 256)
    one1 = P.sb([128, 1], F32)
    P.op("dve", lambda e: e.memset(one1[:], 1.0), writes=[one1])
    eps5 = P.sb([128, 1], F32)
    P.op("dve", lambda e: e.memset(eps5[:], 1e-5), writes=[eps5])
    St = [P.sb([128, 256], F32) for _ in range(NH)]
    Sb = [P.sb([128, 256], BF16) for _ in range(NH)]
    for h in range(NH):
        P.op("dve", lambda e, h=h: e.memset(St[h][:], 0.0), writes=[St[h]])
        P.op("pool", lambda e, h=h: e.memset(Sb[h][:], 0.0), writes=[Sb[h]])
    hf = [P.sb([128, D], BF16) for _ in range(2)]
    hT = P.sb([128, 8, 128], BF16)
    a1T = P.sb([16, 128], BF16)
    la = P.sb([128, QW], F32)
    Eq = P.sb([128, NH, 128], F32)
    Ek = P.sb([128, NH, 128], F32)
    Er = P.sb([128, NH, 128], F32)
    qd = P.sb([128, NH, 128], BF16)
    kd = P.sb([128, NH, 128], BF16)
    kr = P.sb([128, NH, 128], BF16)
    vb = P.sb([128, VW], BF16)
    sr = P.sb([128, VW], F32)
    att = P.sb([128, NH, 128], BF16)
    osb = P.sb([128, NH, 256], F32)
    sq = P.sb([128, NH, 256], F32)
    rs = P.sb([128, 8], F32)
    yo = [P.sb([128, NH, 256], F32) for _ in range(2)]
    yo16 = [P.sb([128, NH, 256], BF16) for _ in range(2)]
    B = P.banks
    pT, pqk, pkt, pz, pv, pr, pcum, po = B[0], B[1], B[2], B[3], B[4], B[5], B[6], B[7]
    patt, pkv = B[0], B[4]

    def load(ci):
        P.dma("sp", hf[ci % 2], hf[ci % 2][:], h_d, h_d[ci * 128:(ci + 1) * 128, :])

    def mm8(pb_t, out_ap, lhs_fn, rhs_fn, reads):
        for c in range(8):
            P.op("pe", lambda e, c=c: e.matmul(out_ap, lhsT=lhs_fn(c), rhs=rhs_fn(c), start=(c == 0), stop=(c == 7)),
                 reads=reads, writes=[pb_t], inc=(c == 7))

    load(0)
    for ci in range(nch):
        if ci + 1 < nch:
            load(ci + 1)
        hfi = hf[ci % 2]
        transpose_to(P, hfi, hfi, 8, ident, pT, hT, hT[:])
        for which in range(2):
            for h in range(NH):
                col = which * QW + h * 128
                slot = (which * NH + h) * 128
                mm8(pqk, pqk[:, slot:slot + 128], lambda c, col=col: w4[:, c, col:col + 128], lambda c: hT[:, c, :], [w4, hT])
        mm8(pkt, pkt[:, 0:QW], lambda c: hT[:, c, :], lambda c: w4[:, c, QW:2 * QW], [w4, hT])
        mm8(pv, pv[:, 0:VW], lambda c: hT[:, c, :], lambda c: w4[:, c, 2 * QW:2 * QW + VW], [w4, hT])
        mm8(pr, pr[:, 0:VW], lambda c: hT[:, c, :], lambda c: w4[:, c, 2 * QW + VW:2 * QW + 2 * VW], [w4, hT])
        mm8(pz, pz[0:16, 384:512], lambda c: wa1[:, c, :], lambda c: hT[:, c, :], [wa1, hT])
        P.op("act", lambda e: e.activation(out=a1T[:], in_=pz[0:16, 384:512], func=AF.Copy), reads=[pz], writes=[a1T])
        P.op("pe", lambda e: e.matmul(pz[:, 0:QW], lhsT=a1T[:], rhs=wa2[:], start=True, stop=False), reads=[a1T, wa2], writes=[pz], inc=False)
        P.op("pe", lambda e: e.matmul(pz[:, 0:QW], lhsT=ones[:], rhs=ba[:], start=False, stop=True), reads=[ones, ba], writes=[pz])
        P.op("act", lambda e: e.activation(out=sr[:], in_=pr[:, 0:VW], func=AF.Sigmoid), reads=[pr], writes=[sr])
        P.op("dve", lambda e: e.tensor_tensor(out=sr[:], in0=sr[:], in1=pr[:, 0:VW], op=ALU.mult), reads=[sr, pr], writes=[sr])
        P.op("act", lambda e: e.activation(out=la[:], in_=pz[:, 0:QW], func=AF.Exp, scale=-1.0), reads=[pz], writes=[la])
        P.op("act", lambda e: e.activation(out=la[:], in_=la[:], func=AF.Ln, bias=one1[:, 0:1]), reads=[la, one1], writes=[la])
        P.op("act", lambda e: e.activation(out=vb[:], in_=pv[:, 0:VW], func=AF.Copy), reads=[pv], writes=[vb])
        for h in range(NH):
            P.op("pe", lambda e, h=h: e.matmul(pcum[:, h * 128:(h + 1) * 128], lhsT=la[:, h * 128:(h + 1) * 128], rhs=trii[:], start=True, stop=True),
                 reads=[la, trii], writes=[pcum], inc=False)
            P.op("pe", lambda e, h=h: e.matmul(pcum[:, (NH + h) * 128:(NH + h + 1) * 128], lhsT=tris[:], rhs=la[:, h * 128:(h + 1) * 128], start=True, stop=True),
                 reads=[la, tris], writes=[pcum], inc=(h == NH - 1))
        cq = pcum[:, 0:NH * 128].rearrange("p (h t) -> p h t", h=NH)
        cr = pcum[:, NH * 128:2 * NH * 128].rearrange("p (h t) -> p h t", h=NH)
        P.op("act", lambda e: e.activation(out=Eq[:], in_=cq, func=AF.Exp), reads=[pcum], writes=[Eq])
        P.op("act", lambda e: e.activation(out=Ek[:], in_=cq, func=AF.Exp, scale=-1.0), reads=[pcum], writes=[Ek])
        P.op("act", lambda e: e.activation(out=Er[:], in_=cr, func=AF.Exp), reads=[pcum], writes=[Er])
        qv = pqk[:, 0:NH * 128].rearrange("p (h t) -> p h t", h=NH)
        kv_ = pqk[:, NH * 128:2 * NH * 128].rearrange("p (h t) -> p h t", h=NH)
        P.op("dve", lambda e: e.scalar_tensor_tensor(out=qd[:], in0=qv, scalar=128.0 ** -0.5, in1=Eq[:], op0=ALU.mult, op1=ALU.mult),
             reads=[pqk, Eq], writes=[qd])
        P.op("dve", lambda e: e.tensor_tensor(out=kd[:], in0=kv_, in1=Ek[:], op=ALU.mult), reads=[pqk, Ek], writes=[kd])
        P.op("dve", lambda e: e.tensor_tensor(out=kr[:], in0=pkt[:, 0:QW].rearrange("p (h t) -> p h t", h=NH), in1=Er[:], op=ALU.mult),
             reads=[pkt, Er], writes=[kr])
        for h in range(NH):
            P.op("pe", lambda e, h=h: e.matmul(patt[:, h * 128:(h + 1) * 128], lhsT=kd[:, h, :], rhs=qd[:, h, :], start=True, stop=True),
                 reads=[kd, qd], writes=[patt], inc=(h == NH - 1))
        P.op("dve", lambda e: e.tensor_tensor(out=att[:], in0=patt[:, 0:NH * 128].rearrange("p (h t) -> p h t", h=NH),
                                              in1=mask[:].unsqueeze(1).to_broadcast([128, NH, 128]), op=ALU.mult),
             reads=[patt, mask], writes=[att])
        for h in range(NH):
            P.op("pe", lambda e, h=h: e.matmul(po[:, h * 256:(h + 1) * 256], lhsT=att[:, h, :], rhs=vb[:, h * 256:(h + 1) * 256], start=True, stop=False),
                 reads=[att, vb], writes=[po], inc=False)
            P.op("pe", lambda e, h=h: e.matmul(po[:, h * 256:(h + 1) * 256], lhsT=qd[:, h, :], rhs=Sb[h][:], start=False, stop=True),
                 reads=[qd, Sb[h]], writes=[po], inc=(h == NH - 1))
        for h in range(NH):
            P.op("pe", lambda e, h=h: e.matmul(pkv[:, h * 256:(h + 1) * 256], lhsT=kr[:, h, :], rhs=vb[:, h * 256:(h + 1) * 256], start=True, stop=True),
                 reads=[kr, vb], writes=[pkv], inc=(h == NH - 1))
        for h in range(NH):
            P.op("dve", lambda e, h=h: e.scalar_tensor_tensor(out=St[h][:], in0=St[h][:], scalar=Eq[:, h, 127:128], in1=pkv[:, h * 256:(h + 1) * 256],
                                                             op0=ALU.mult, op1=ALU.add), reads=[St[h], Eq, pkv], writes=[St[h]])
            P.op("pool", lambda e, h=h: e.tensor_copy(out=Sb[h][:], in_=St[h][:]), reads=[St[h]], writes=[Sb[h]])
        y = yo[ci % 2]
        P.op("act", lambda e: e.activation(out=osb[:], in_=po[:, 0:VW].rearrange("p (h d) -> p h d", h=NH), func=AF.Copy), reads=[po], writes=[osb])
        P.op("pool", lambda e: e.tensor_tensor(out=sq[:], in0=osb[:], in1=osb[:], op=ALU.mult), reads=[osb], writes=[sq])
        P.op("dve", lambda e: e.tensor_reduce(out=rs[:, 0:NH], in_=sq[:], axis=AX.X, op=ALU.add), reads=[sq], writes=[rs])
        P.op("act", lambda e: e.activation(out=rs[:, 0:NH], in_=rs[:, 0:NH], func=AF.Ln, bias=eps5[:, 0:1], scale=1.0 / 256.0), reads=[rs, eps5], writes=[rs])
        P.op("act", lambda e: e.activation(out=rs[:, 4:4 + NH], in_=rs[:, 0:NH], func=AF.Exp, scale=-0.5), reads=[rs], writes=[rs])
        P.op("pool", lambda e, y=y: e.tensor_tensor(out=y[:], in0=osb[:], in1=rs[:, 4:4 + NH].unsqueeze(2).to_broadcast([128, NH, 256]), op=ALU.mult),
             reads=[osb, rs], writes=[y])
        P.op("pool", lambda e, y=y: e.tensor_tensor(out=y[:], in0=y[:], in1=ng[:].unsqueeze(1).to_broadcast([128, NH, 256]), op=ALU.mult),
             reads=[y, ng], writes=[y])
        y16 = yo16[ci % 2]
        P.op("dve", lambda e, y=y, y16=y16: e.tensor_tensor(out=y16[:], in0=y[:], in1=sr[:].rearrange("p (h d) -> p h d", h=NH), op=ALU.mult),
             reads=[y, sr], writes=[y16])
        P.dma("pool", o_d, o_d[ci * 128:(ci + 1) * 128, ocol:ocol + VW], y16, y16[:].rearrange("p h d -> p (h d)"), persistent=True)
    P.release(m0)


RW_C = 0.6065306597126334


class _Stop(Exception):
    pass


W_DBG = 0


def phase_W(P, S, NPR, h_d, mix_d, vecsT_d, wr_d, wk_d, wv_d, w1_d, a1_d, g1_d, w2_d, a2_d, g2_d, vecs_d, cst_d, o_d, ocol=0):
    try:
        _phase_W(P, S, NPR, h_d, mix_d, vecsT_d, wr_d, wk_d, wv_d, w1_d, a1_d, g1_d, w2_d, a2_d, g2_d, vecs_d, cst_d, o_d, ocol)
    except _Stop:
        P.barrier()
        P.top = 0


def _phase_W(P, S, NPR, h_d, mix_d, vecsT_d, wr_d, wk_d, wv_d, w1_d, a1_d, g1_d, w2_d, a2_d, g2_d, vecs_d, cst_d, o_d, ocol=0):
    m0 = P.mark()
    CH = NPR * 128
    ng = S // 512
    B = P.banks
    cst = P.sb([128, 6, 128], F32)
    P.dma("sp", cst, cst[:], cst_d, cst_d[0:6].rearrange("a p n -> p a n"))
    identf, MS, MSt, MI, bones, hsel = [cst[:, i, :] for i in range(6)]
    identb = P.sb([128, 128], BF16)
    P.dma("pool", identb, identb[:], cst_d, cst_d[0])
    rmask = P.sb([128, 512], F32)
    P.dma("sp", rmask, rmask[:], cst_d, cst_d[6, 0:4, :].rearrange("a n -> (a n)").partition_broadcast(128))
    vec = P.sb([128, 8, NPR], F32)
    P.dma("sp", vec, vec[:], vecsT_d, vecsT_d.rearrange("p (a c) -> p a c", a=8))
    omka = P.sb([128, NPR], F32)
    P.op("dve", lambda e: e.tensor_scalar(out=omka[:], in0=vec[:, 3, :], scalar1=-1.0, scalar2=1.0, op0=ALU.mult, op1=ALU.add), reads=[vec], writes=[omka])
    lng = P.sb([128, CH], F32)
    P.dma("sp", lng, lng[:], vecs_d, vecs_d[5].partition_broadcast(128))
    lnb = P.sb([128, CH], F32)
    P.dma("sp", lnb, lnb[:], vecs_d, vecs_d[6].partition_broadcast(128))
    mixT = P.sb([128, 6, 8], F32)
    P.dma("sp", mixT, mixT[:], mix_d, mix_d.rearrange("p (g c) -> p g c", g=6))
    omix = P.sb([128, 6, 8], F32)
    P.op("dve", lambda e: e.tensor_scalar(out=omix[:], in0=mixT[:], scalar1=-1.0, scalar2=1.0, op0=ALU.mult, op1=ALU.add), reads=[mixT], writes=[omix])
    NW = 3 * CH + 256
    Wc = P.sb([128, 16, NW], BF16)
    cols = [(wr_d, 0, CH, 0), (wk_d, CH, CH, 1), (wv_d, 2 * CH, CH, 2), (w1_d, 3 * CH, 64, 3), (a1_d, 3 * CH + 64, 64, 4), (g1_d, 3 * CH + 128, 128, 5)]
    ms = P.mark()
    stg = [P.sb([128, 8, 512], F32) for _ in range(2)]
    for wi, (wd, c0, n, gi) in enumerate(cols):
        st = stg[wi % 2]
        P.dma("sp", st, st[:, :, 0:n], wd, wd.rearrange("(c p) n -> p c n", p=128))
        for c in range(8):
            P.op("dve", lambda e, st=st, c=c, c0=c0, n=n, gi=gi: e.tensor_scalar(out=Wc[:, c, c0:c0 + n], in0=st[:, c, 0:n], scalar1=omix[:, gi, c:c + 1],
                                                                               scalar2=None, op0=ALU.mult), reads=[st, omix], writes=[Wc])
            P.op("pool", lambda e, st=st, c=c, c0=c0, n=n, gi=gi: e.tensor_scalar(out=Wc[:, 8 + c, c0:c0 + n], in0=st[:, c, 0:n], scalar1=mixT[:, gi, c:c + 1],
                                                                                scalar2=None, op0=ALU.mult), reads=[st, mixT], writes=[Wc])
    P.release(ms)
    w2 = P.sb([64, CH], BF16)
    P.dma("pool", w2, w2[:], w2_d, w2_d[:])
    a2 = P.sb([64, CH], BF16)
    P.dma("pool", a2, a2[:], a2_d, a2_d[:])
    g2 = P.sb([128, CH], BF16)
    P.dma("pool", g2, g2[:], g2_d, g2_d[:])
    Z = [P.sb([128, 128], F32) for _ in range(NPR)]
    Ucs = [P.sb([128, 128], F32) for _ in range(2)]
    Ktc = [P.sb([128, 128], F32) for _ in range(2)]
    for z in Ucs + Ktc:
        P.op("dve", lambda e, z=z: e.memset(z[:], 0.0), writes=[z])
    for z in Z:
        P.op("dve", lambda e, z=z: e.memset(z[:], 0.0), writes=[z])
    if W_DBG == 1:
        raise _Stop()
    hf = [P.sb([128, 4, D], BF16) for _ in range(2)]
    xT = P.sb([128, 8, 514], BF16)
    P.op("dve", lambda e: e.memset(xT[:, :, 0:2], 0.0), writes=[xT])
    wl = P.sb([64, 512], BF16)
    al = P.sb([64, 512], BF16)
    gl = P.sb([128, 512], BF16)
    F = lambda: P.sb([128, 512], F32)
    sw, aa, Lp, t0, t1, EL, EnL, ELx, Erm, kk, kp, ka = [F() for _ in range(12)]
    rhat, ahat, bhat, khat, nbt, kt, vT, rkr = [F() for _ in range(8)]
    gC = P.sb([128, 8], F32)
    ahat2, rhat2, bhat2 = [P.sb([128, 2, 512], F32) for _ in range(3)]
    for z_ in (ahat2, rhat2, bhat2):
        P.op("pool", lambda e, z_=z_: e.memset(z_[:], 0.0), writes=[z_])
    Vtok, nBt, Kt, At, PP, WW, PT, U, Ysb, Yc, sqy = [P.sb([128, 128], F32) for _ in range(11)]
    Fm = [[P.sb([128, 128], F32) for _ in range(2)] for _ in range(2)]
    X = [P.sb([128, 128], F32) for _ in range(2)]
    NakT, nMrbT, MrkT = [[P.sb([128, 128], F32) for _ in range(2)] for _ in range(3)]
    st8 = P.sb([128, 16], F32)
    bcs = P.sb([128, 2], F32)
    ob = [P.sb([128, 128], BF16) for _ in range(2)]

    def load(g):
        P.dma("sp", hf[g % 2], hf[g % 2][:], h_d, h_d[g * 512:(g + 1) * 512, :].rearrange("(j p) n -> p j n", p=128))

    def tr32(src_t, src_ap, dst_t, dst_ap, eng="act"):
        P.op("pe", lambda e: e.transpose(out=B[0][:, 0:128], in_=src_ap, identity=identf), reads=[src_t, cst], writes=[B[0]])
        if eng == "act":
            P.op("act", lambda e: e.activation(out=dst_ap, in_=B[0][:, 0:128], func=AF.Copy), reads=[B[0]], writes=[dst_t])
        else:
            P.op("dve", lambda e: e.tensor_copy(out=dst_ap, in_=B[0][:, 0:128]), reads=[B[0]], writes=[dst_t])

    def proj(pb, out_ap, c0, n, reads_extra=()):
        for c in range(16):
            rhs = xT[:, c, 1:513] if c < 8 else xT[:, c - 8, 0:512]
            P.op("pe", lambda e, c=c, rhs=rhs: e.matmul(out_ap, lhsT=Wc[:, c, c0:c0 + n], rhs=rhs, start=(c == 0), stop=(c == 15)),
                 reads=[Wc, xT], writes=[pb], inc=(c == 15))

    def V_(fn, r, w):
        P.op("dve", fn, reads=r, writes=w)

    def A_(fn, r, w):
        P.op("act", fn, reads=r, writes=w)

    load(0)
    for g in range(ng):
        if g + 1 < ng:
            load(g + 1)
        if g > 0:
            V_(lambda e: e.tensor_copy(out=xT[:, :, 0:1], in_=xT[:, :, 512:513]), [xT], [xT])
        for j in range(4):
            transpose_to(P, hf[g % 2], hf[g % 2][:, j, :], 8, identb, B[0], xT, xT[:, :, 1 + j * 128:1 + (j + 1) * 128])
        proj(B[6], B[6][0:64, :], 3 * CH, 64)
        A_(lambda e: e.activation(out=wl[:], in_=B[6][0:64, :], func=AF.Tanh), [B[6]], [wl])
        proj(B[6], B[6][0:64, :], 3 * CH + 64, 64)
        A_(lambda e: e.activation(out=al[:], in_=B[6][0:64, :], func=AF.Copy), [B[6]], [al])
        proj(B[6], B[6][:, :], 3 * CH + 128, 128)
        A_(lambda e: e.activation(out=gl[:], in_=B[6][:, :], func=AF.Sigmoid), [B[6]], [gl])
        if W_DBG == 2:
            raise _Stop()
        for oc in range(NPR):
            osl = slice(oc * 128, (oc + 1) * 128)
            vcol = lambda i, oc=oc: vec[:, i, oc:oc + 1]
            rP, kP, vP = B[1], B[2], B[3]
            proj(rP, rP[:], oc * 128, 128)
            proj(kP, kP[:], CH + oc * 128, 128)
            proj(vP, vP[:], 2 * CH + oc * 128, 128)
            P.op("pe", lambda e, osl=osl: e.matmul(B[4][:], lhsT=w2[:, osl], rhs=wl[:], start=True, stop=True), reads=[w2, wl], writes=[B[4]])
            A_(lambda e, vcol=vcol: e.activation(out=sw[:], in_=B[4][:], func=AF.Sigmoid, bias=vcol(0)), [B[4], vec], [sw])
            P.op("pe", lambda e, osl=osl: e.matmul(B[4][:], lhsT=a2[:, osl], rhs=al[:], start=True, stop=True), reads=[a2, al], writes=[B[4]])
            A_(lambda e, vcol=vcol: e.activation(out=aa[:], in_=B[4][:], func=AF.Sigmoid, bias=vcol(1)), [B[4], vec], [aa])
            A_(lambda e: e.activation(out=vT[:], in_=vP[:], func=AF.Copy), [vP], [vT])
            V_(lambda e: e.tensor_tensor_scan(out=Lp[:], data0=rmask[:], data1=sw[:], initial=0.0, op0=ALU.mult, op1=ALU.add), [rmask, sw], [Lp])
            V_(lambda e: e.tensor_tensor(out=t0[:], in0=Lp[:], in1=sw[:], op=ALU.subtract), [Lp, sw], [t0])
            Lp3 = Lp[:].rearrange("p (c t) -> p c t", c=8)
            V_(lambda e: e.tensor_tensor(out=t1[:].rearrange("p (c t) -> p c t", c=8), in0=Lp3[:, :, 63:64].to_broadcast([128, 8, 64]), in1=Lp3, op=ALU.subtract),
               [Lp], [t1])
            A_(lambda e: e.activation(out=EL[:], in_=Lp[:], func=AF.Exp, scale=-RW_C), [Lp], [EL])
            A_(lambda e: e.activation(out=EnL[:], in_=Lp[:], func=AF.Exp, scale=RW_C), [Lp], [EnL])
            A_(lambda e: e.activation(out=ELx[:], in_=t0[:], func=AF.Exp, scale=-RW_C), [t0], [ELx])
            A_(lambda e: e.activation(out=Erm[:], in_=t1[:], func=AF.Exp, scale=-RW_C), [t1], [Erm])
            V_(lambda e: e.tensor_copy(out=gC[:], in_=EL[:].rearrange("p (c t) -> p c t", c=8)[:, :, 63]), [EL], [gC])
            V_(lambda e, vcol=vcol: e.tensor_scalar(out=kk[:], in0=kP[:], scalar1=vcol(2), scalar2=None, op0=ALU.mult), [kP, vec], [kk])
            P.op("pool", lambda e: e.tensor_tensor(out=t0[:], in0=kk[:], in1=kk[:], op=ALU.mult), reads=[kk], writes=[t0])
            P.op("pe", lambda e: e.matmul(B[5][:], lhsT=bones, rhs=t0[:], start=True, stop=True), reads=[cst, t0], writes=[B[5]])
            A_(lambda e: e.activation(out=t1[:], in_=B[5][:], func=AF.Sqrt, bias=1e-30), [B[5]], [t1])
            V_(lambda e: e.reciprocal(out=t1[:], in_=t1[:]), [t1], [t1])
            V_(lambda e: e.tensor_tensor(out=kk[:], in0=kk[:], in1=t1[:], op=ALU.mult), [kk, t1], [kk])
            V_(lambda e, vcol=vcol, oc=oc: e.tensor_scalar(out=t0[:], in0=aa[:], scalar1=vcol(3), scalar2=omka[:, oc:oc + 1], op0=ALU.mult, op1=ALU.add),
               [aa, vec, omka], [t0])
            V_(lambda e: e.tensor_tensor(out=kp[:], in0=kP[:], in1=t0[:], op=ALU.mult), [kP, t0], [kp])
            V_(lambda e: e.tensor_tensor(out=rhat[:], in0=rP[:], in1=EL[:], op=ALU.mult), [rP, EL], [rhat])
            V_(lambda e, vcol=vcol: e.scalar_tensor_tensor(out=rkr[:], in0=rP[:], scalar=vcol(4), in1=kp[:], op0=ALU.mult, op1=ALU.mult), [rP, vec, kp], [rkr])
            P.op("pool", lambda e: e.tensor_tensor(out=ahat[:], in0=kk[:], in1=ELx[:], op=ALU.mult), reads=[kk, ELx], writes=[ahat])
            P.op("pool", lambda e: e.tensor_tensor(out=ka[:], in0=kk[:], in1=aa[:], op=ALU.mult), reads=[kk, aa], writes=[ka])
            P.op("pool", lambda e: e.tensor_tensor(out=bhat[:], in0=ka[:], in1=EnL[:], op=ALU.mult), reads=[ka, EnL], writes=[bhat])
            V_(lambda e: e.scalar_tensor_tensor(out=nbt[:], in0=ka[:], scalar=-1.0, in1=Erm[:], op0=ALU.mult, op1=ALU.mult), [ka, Erm], [nbt])
            P.op("pool", lambda e: e.tensor_tensor(out=khat[:], in0=kp[:], in1=EnL[:], op=ALU.mult), reads=[kp, EnL], writes=[khat])
            P.op("pool", lambda e: e.tensor_tensor(out=kt[:], in0=kp[:], in1=Erm[:], op=ALU.mult), reads=[kp, Erm], writes=[kt])
            for src, dst in ((ahat, ahat2), (rhat, rhat2), (bhat, bhat2)):
                A_(lambda e, src=src, dst=dst: e.activation(out=dst[0:64, 0, :], in_=src[0:64, :], func=AF.Copy), [src], [dst])
                A_(lambda e, src=src, dst=dst: e.activation(out=dst[64:128, 1, :], in_=src[64:128, :], func=AF.Copy), [src], [dst])
            if W_DBG == 3:
                raise _Stop()
            for blk in range(4):
                ts = slice(blk * 128, (blk + 1) * 128)
                tr32(vT, vT[:, ts], Vtok, Vtok[:])
                tr32(nbt, nbt[:, ts], nBt, nBt[:], "dve")
                P.op("pe", lambda e, ts=ts: e.transpose(out=B[0][:, 0:128], in_=kt[:, ts], identity=identf), reads=[kt, cst], writes=[B[0]])
                A_(lambda e: e.activation(out=Ktc[0][0:64, :], in_=B[0][0:64, 0:128], func=AF.Copy), [B[0]], [Ktc[0]])
                A_(lambda e: e.activation(out=Ktc[1][64:128, :], in_=B[0][64:128, 0:128], func=AF.Copy), [B[0]], [Ktc[1]])
                tr32(ahat, ahat[:, ts], At, At[:], "dve")
                if W_DBG == 4:
                    raise _Stop()
                G0, G1, G2 = B[1], B[2], B[4]

                def mmg(out_ap, l, r2, pb, inc, ts=ts):
                    P.op("pe", lambda e: e.matmul(out_ap, lhsT=l[:, ts], rhs=r2[:, :, ts], start=True, stop=True), reads=[l, r2], writes=[pb], inc=inc)
                mmg(G0[:, 0:256], bhat, ahat2, G0, False)
                mmg(G0[:, 256:512], ahat, bhat2, G0, True)
                mmg(G1[:, 0:256], khat, ahat2, G1, False)
                mmg(G1[:, 256:512], bhat, rhat2, G1, True)
                mmg(G2[:, 0:256], khat, rhat2, G2, True)
                for h in range(2):
                    hs = slice(h * 64, (h + 1) * 64)
                    c0_, c1_ = h * 128, 256 + h * 128
                    Fc, FTc = Fm[0]
                    V_(lambda e, Fc=Fc, c0_=c0_: e.scalar_tensor_tensor(out=Fc[:], in0=G0[:, c0_:c0_ + 128], scalar=-1.0, in1=MS, op0=ALU.mult, op1=ALU.mult), [G0, cst], [Fc])
                    V_(lambda e, FTc=FTc, c1_=c1_: e.scalar_tensor_tensor(out=FTc[:], in0=G0[:, c1_:c1_ + 128], scalar=-1.0, in1=MSt, op0=ALU.mult, op1=ALU.mult), [G0, cst], [FTc])
                    V_(lambda e, h=h, c0_=c0_: e.tensor_tensor(out=NakT[h][:], in0=G1[:, c0_:c0_ + 128], in1=MS, op=ALU.mult), [G1, cst], [NakT[h]])
                    V_(lambda e, h=h, c1_=c1_: e.scalar_tensor_tensor(out=nMrbT[h][:], in0=G1[:, c1_:c1_ + 128], scalar=-1.0, in1=MI, op0=ALU.mult, op1=ALU.mult), [G1, cst], [nMrbT[h]])
                    V_(lambda e, h=h, c0_=c0_: e.tensor_tensor(out=MrkT[h][:], in0=G2[:, c0_:c0_ + 128], in1=MI, op=ALU.mult), [G2, cst], [MrkT[h]])
                    if W_DBG == 5:
                        raise _Stop()
                    P.op("pe", lambda e, h=h, hs=hs: e.matmul(B[7][:, 0:64], lhsT=NakT[h][:], rhs=Vtok[:, hs], start=True, stop=True),
                         reads=[NakT[h], Vtok], writes=[B[7]])
                    A_(lambda e, hs=hs: e.activation(out=X[0][:, 0:64], in_=At[:, hs], func=AF.Copy), [At], [X[0]])
                    A_(lambda e: e.activation(out=X[0][:, 64:128], in_=B[7][:, 0:64], func=AF.Copy), [B[7]], [X[0]])
                    for lv in range(6):
                        Fc, FTc = Fm[lv % 2]
                        Fn, FTn = Fm[(lv + 1) % 2]
                        Xc, Xn = X[lv % 2], X[(lv + 1) % 2]
                        P.op("pe", lambda e, Fc=Fc, Xc=Xc: e.matmul(B[3][:, 0:128], lhsT=Fc[:], rhs=Xc[:], start=True, stop=True), reads=[Fc, Xc], writes=[B[3]])
                        if lv < 5:
                            P.op("pe", lambda e, Fc=Fc, FTc=FTc: e.matmul(B[7][:, 0:128], lhsT=FTc[:], rhs=Fc[:], start=True, stop=True),
                                 reads=[Fc, FTc], writes=[B[7]], inc=False)
                            P.op("pe", lambda e, Fc=Fc, FTc=FTc: e.matmul(B[7][:, 128:256], lhsT=Fc[:], rhs=FTc[:], start=True, stop=True),
                                 reads=[Fc, FTc], writes=[B[7]])
                            V_(lambda e, Xc=Xc, Xn=Xn: e.tensor_tensor(out=Xn[:], in0=B[3][:, 0:128], in1=Xc[:], op=ALU.add), [B[3], Xc], [Xn])
                            A_(lambda e, Fn=Fn: e.activation(out=Fn[:], in_=B[7][:, 0:128], func=AF.Copy), [B[7]], [Fn])
                            A_(lambda e, FTn=FTn: e.activation(out=FTn[:], in_=B[7][:, 128:256], func=AF.Copy), [B[7]], [FTn])
                        else:
                            V_(lambda e, Xc=Xc, hs=hs: e.tensor_tensor(out=PP[:, hs], in0=B[3][:, 0:64], in1=Xc[:, 0:64], op=ALU.add), [B[3], Xc], [PP])
                            V_(lambda e, Xc=Xc, hs=hs: e.tensor_tensor(out=WW[:, hs], in0=B[3][:, 64:128], in1=Xc[:, 64:128], op=ALU.add), [B[3], Xc], [WW])
                if W_DBG == 6:
                    raise _Stop()
                tr32(PP, PP[:], PT, PT[:])
                z = Z[oc]
                pU, pY, pZ = B[4], B[5], B[6]
                for cp in range(2):
                    rt = slice(cp * 64, (cp + 1) * 64)
                    Uc = Ucs[cp]
                    P.op("pe", lambda e, z=z: e.matmul(pU[:, 0:128], lhsT=PT[:], rhs=z[:], start=True, stop=True), reads=[PT, z], writes=[pU])
                    V_(lambda e, rt=rt, Uc=Uc: e.tensor_tensor(out=Uc[rt, :], in0=pU[rt, 0:128], in1=WW[rt, :], op=ALU.add), [pU, WW], [Uc])
                    P.op("pe", lambda e, ts=ts, z=z: e.matmul(pY[:, 0:128], lhsT=rhat[:, ts], rhs=z[:], start=True, stop=False), reads=[rhat, z], writes=[pY], inc=False)
                    for h in range(2):
                        hs = slice(h * 64, (h + 1) * 64)
                        P.op("pe", lambda e, hs=hs, h=h, Uc=Uc: e.matmul(pY[:, hs], lhsT=nMrbT[h][:], rhs=Uc[:, hs], start=False, stop=False),
                             reads=[nMrbT[h], Uc], writes=[pY], inc=False)
                        P.op("pe", lambda e, hs=hs, h=h: e.matmul(pY[:, hs], lhsT=MrkT[h][:], rhs=Vtok[:, hs], start=False, stop=(h == 1)),
                             reads=[MrkT[h], Vtok], writes=[pY], inc=(h == 1))
                    A_(lambda e, rt=rt: e.activation(out=Ysb[rt, :], in_=pY[rt, 0:128], func=AF.Copy), [pY], [Ysb])
                    for h in range(2):
                        hs = slice(h * 64, (h + 1) * 64)
                        P.op("pe", lambda e, hs=hs, Uc=Uc: e.matmul(pZ[:, hs], lhsT=nBt[:], rhs=Uc[:, hs], start=True, stop=False),
                             reads=[nBt, Uc], writes=[pZ], inc=False)
                        P.op("pe", lambda e, hs=hs, cp=cp: e.matmul(pZ[:, hs], lhsT=Ktc[cp][:], rhs=Vtok[:, hs], start=False, stop=True),
                             reads=[Ktc[cp], Vtok], writes=[pZ], inc=(h == 1))
                    ci = blk * 2 + cp
                    for h in range(2):
                        hs = slice(h * 64, (h + 1) * 64)
                        V_(lambda e, ci=ci, hs=hs, z=z: e.scalar_tensor_tensor(out=z[hs, hs], in0=z[hs, hs], scalar=gC[hs, ci:ci + 1], in1=pZ[hs, hs],
                                                                        op0=ALU.mult, op1=ALU.add), [z, gC, pZ], [z])
                if W_DBG == 7:
                    raise _Stop()
                P.op("pe", lambda e, ts=ts: e.matmul(B[7][:, 0:2], lhsT=rkr[:, ts], rhs=hsel[:, 0:2], start=True, stop=True), reads=[rkr, cst], writes=[B[7]], inc=False)
                P.op("pe", lambda e, ts=ts, osl=osl: e.matmul(B[7][:, 128:256], lhsT=gl[:, ts], rhs=g2[:, osl], start=True, stop=True), reads=[gl, g2], writes=[B[7]])
                Y3 = Ysb[:].rearrange("p (h v) -> p h v", h=2)
                Yc3 = Yc[:].rearrange("p (h v) -> p h v", h=2)
                V_(lambda e: e.tensor_reduce(out=st8[:, 0:2], in_=Y3, axis=AX.X, op=ALU.add), [Ysb], [st8])
                V_(lambda e: e.tensor_scalar(out=st8[:, 2:4], in0=st8[:, 0:2], scalar1=1.0 / 64.0, scalar2=None, op0=ALU.mult), [st8], [st8])
                V_(lambda e: e.tensor_tensor(out=Yc3, in0=Y3, in1=st8[:, 2:4].unsqueeze(2).to_broadcast([128, 2, 64]), op=ALU.subtract), [Ysb, st8], [Yc])
                P.op("pool", lambda e: e.tensor_tensor(out=sqy[:], in0=Yc[:], in1=Yc[:], op=ALU.mult), reads=[Yc], writes=[sqy])
                V_(lambda e: e.tensor_reduce(out=st8[:, 4:6], in_=sqy[:].rearrange("p (h v) -> p h v", h=2), axis=AX.X, op=ALU.add), [sqy], [st8])
                A_(lambda e: e.activation(out=st8[:, 6:8], in_=st8[:, 4:6], func=AF.Sqrt, bias=64e-5, scale=1.0 / 64.0), [st8], [st8])
                V_(lambda e: e.reciprocal(out=st8[:, 8:10], in_=st8[:, 6:8]), [st8], [st8])
                V_(lambda e: e.tensor_tensor(out=Yc3, in0=Yc3, in1=st8[:, 8:10].unsqueeze(2).to_broadcast([128, 2, 64]), op=ALU.mult), [Yc, st8], [Yc])
                P.op("pool", lambda e, osl=osl: e.tensor_tensor(out=Yc[:], in0=Yc[:], in1=lng[:, osl], op=ALU.mult), reads=[Yc, lng], writes=[Yc])
                P.op("pool", lambda e, osl=osl: e.tensor_tensor(out=Yc[:], in0=Yc[:], in1=lnb[:, osl], op=ALU.add), reads=[Yc, lnb], writes=[Yc])
                V_(lambda e: e.tensor_copy(out=bcs[:], in_=B[7][:, 0:2]), [B[7]], [bcs])
                P.op("pool", lambda e: e.tensor_tensor(out=sqy[:].rearrange("p (h v) -> p h v", h=2), in0=Vtok[:].rearrange("p (h v) -> p h v", h=2),
                                                       in1=bcs[:].unsqueeze(2).to_broadcast([128, 2, 64]), op=ALU.mult), reads=[Vtok, bcs], writes=[sqy])
                P.op("pool", lambda e: e.tensor_tensor(out=Yc[:], in0=Yc[:], in1=sqy[:], op=ALU.add), reads=[Yc, sqy], writes=[Yc])
                o = ob[(oc * 4 + blk) % 2]
                V_(lambda e, o=o: e.tensor_tensor(out=o[:], in0=Yc[:], in1=B[7][:, 128:256], op=ALU.mult), [Yc, B[7]], [o])
                r0 = g * 512 + blk * 128
                P.dma("pool", o_d, o_d[r0:r0 + 128, ocol + oc * 128:ocol + (oc + 1) * 128], o, o[:], persistent=True)
    P.release(m0)


def rwkv_consts():
    ar = np.arange(128)
    same = (ar[:, None] // 64) == (ar[None, :] // 64)
    c = np.zeros((8, 128, 128), np.float32)
    c[0] = np.eye(128)
    c[1] = same & (ar[:, None] < ar[None, :])
    c[2] = c[1].T
    c[3] = same & (ar[:, None] <= ar[None, :])
    c[4] = same
    c[5, :64, 0] = 1
    c[5, 64:, 1] = 1
    rm = np.ones(512, np.float32)
    rm[::64] = 0
    c[6, 0:4, :] = rm.reshape(4, 128)
    return c


def rwkv_vec_layouts(mix, vecs):
    mixT = np.ascontiguousarray(mix.reshape(6, 8, 128).transpose(2, 0, 1)).reshape(128, 48)
    ch = vecs.shape[1]
    vT = np.ascontiguousarray(vecs.reshape(8, ch // 128, 128).transpose(2, 0, 1)).reshape(128, 8 * (ch // 128))
    return mixT.astype(np.float32), vT.astype(np.float32)


def lam_init_of(i):
    return 0.8 - 0.6 * math.exp(-0.3 * i)


def layer_inputs(inp, i):
    kind, j = i % 3, i // 3
    d = {}
    pre = "L%d_" % i
    for nm, key in (("g1", "ln1_g"), ("b1", "ln1_b"), ("g2", "ln2_g"), ("b2", "ln2_b"), ("wup", "mlp_up"), ("wdn", "mlp_down"),
                    ("wg", "ple_gate"), ("wp", "ple_proj")):
        d[pre + nm] = np.ascontiguousarray(inp[key][i])
    if kind == 0:
        w = inp["da_w_qkv"][j]
        for hh in range(2):
            sl = slice(hh * 512, (hh + 1) * 512)
            d[pre + "w3_%d" % hh] = np.ascontiguousarray(np.concatenate([w[:, 0:1024][:, sl], w[:, 1024:2048][:, sl], w[:, 2048:3072][:, sl]], 1))
            heads = list(range(hh * 4, hh * 4 + 4))
            d[pre + "btab_%d" % hh] = attn_bias_tables(inp["rel_bias"], heads)
            d[pre + "b31_%d" % hh] = np.ascontiguousarray(inp["rel_bias"][31, heads])
        d[pre + "lam"] = np.stack([inp["da_lam_q1"][j], inp["da_lam_k1"][j], inp["da_lam_q2"][j], inp["da_lam_k2"][j]]).astype(np.float32)
        d[pre + "subg"] = np.ascontiguousarray(inp["da_subln_g"][j])
        d[pre + "wo"] = np.ascontiguousarray(inp["da_w_o"][j])
    elif kind == 1:
        w = inp["gla_w_in"][j]
        for hh in range(2):
            d[pre + "w4_%d" % hh] = np.ascontiguousarray(np.concatenate(
                [w[:, hh * 256:(hh + 1) * 256], w[:, 512 + hh * 256:512 + (hh + 1) * 256],
                 w[:, 1024 + hh * 512:1024 + (hh + 1) * 512], w[:, 2048 + hh * 512:2048 + (hh + 1) * 512]], 1))
            d[pre + "wa2_%d" % hh] = np.ascontiguousarray(inp["gla_w_a2"][j][:, hh * 256:(hh + 1) * 256])
            d[pre + "ba_%d" % hh] = np.ascontiguousarray(inp["gla_b_a"][j][hh * 256:(hh + 1) * 256])
        d[pre + "wa1"] = np.ascontiguousarray(inp["gla_w_a1"][j])
        d[pre + "ng"] = np.ascontiguousarray(inp["gla_norm_g"][j])
        d[pre + "wo"] = np.ascontiguousarray(inp["gla_w_o"][j])
    else:
        for hh in range(2):
            sl = slice(hh * 512, (hh + 1) * 512)
            for nm, a in (("wr", inp["rw_w_rkv"][j][0]), ("wk", inp["rw_w_rkv"][j][1]), ("wv", inp["rw_w_rkv"][j][2]),
                          ("w2", inp["rw_w2"][j]), ("a2", inp["rw_a2"][j]), ("g2", inp["rw_g2"][j])):
                d[pre + nm + "_%d" % hh] = np.ascontiguousarray(a[:, sl])
            vecs = np.stack([inp["rw_w0"][j][sl], inp["rw_a0"][j][sl], inp["rw_k_k"][j][sl], inp["rw_k_a"][j][sl],
                             inp["rw_r_k"][j].reshape(-1)[sl], inp["rw_ln_g"][j][sl], inp["rw_ln_b"][j][sl],
                             inp["rw_ln_b"][j][sl]]).astype(np.float32)
            mixT, vT = rwkv_vec_layouts(inp["rw_mix"][j], vecs)
            d[pre + "vecs_%d" % hh] = vecs
            d[pre + "vecsT_%d" % hh] = vT
            d[pre + "mixT"] = mixT
        d[pre + "rw1"] = np.ascontiguousarray(inp["rw_w1"][j])
        d[pre + "ra1"] = np.ascontiguousarray(inp["rw_a1"][j])
        d[pre + "rg1"] = np.ascontiguousarray(inp["rw_g1"][j])
        d[pre + "wo"] = np.ascontiguousarray(inp["rw_w_o"][j])
    return d


def const_inputs():
    ar = np.arange(128)
    return {"ident": np.eye(128, dtype=np.float32),
            "mask": (ar[None, :] >= ar[:, None]).astype(np.float32),
            "tris": (ar[:, None] > ar[None, :]).astype(np.float32),
            "rwcst": rwkv_consts()}


PHASE_SEL = ("mix", "Ra", "Rb", "Rc")


def build_program(layers, shapes, S=SEQ):
    nc = bass.Bass("TRN2", target_bir_lowering=False)
    P = Prog(nc)
    IN = {}
    for name, shp in shapes.items():
        IN[name] = P.dram(name, list(shp), F32, kind="ExternalInput")
    out = P.dram("out", [S, D], F32, kind="ExternalOutput")
    o_d = P.dram("scr_o", [S, D], BF16)
    h1_d = P.dram("scr_h1", [S, D], F32)
    h1b_d = P.dram("scr_h1b", [S, D], BF16)
    h2_d = P.dram("scr_h2", [S, D], F32)
    h2b_d = P.dram("scr_h2b", [S, D], BF16)
    hbuf = [P.dram("scr_ha", [S, D], F32), P.dram("scr_hb", [S, D], F32)]
    hbb = [P.dram("scr_hab", [S, D], BF16), P.dram("scr_hbb", [S, D], BF16), P.dram("scr_xb", [S, D], BF16)]
    h_in = IN["x"]
    hb_in = hbb[2]
    phase_X(P, S, h_in, hb_in)
    for n, i in enumerate(layers):
        kind = i % 3
        pre = "L%d_" % i
        g = lambda nm: IN[pre + nm]
        for hh in range(2):
            if "mix" not in PHASE_SEL:
                break
            if kind == 0:
                phase_A(P, S, 4, lam_init_of(i), hb_in, g("w3_%d" % hh), g("btab_%d" % hh), IN["mask"], g("b31_%d" % hh), g("lam"), g("subg"),
                        IN["ident"], o_d, ocol=hh * 512)
            elif kind == 1:
                phase_G(P, S, 2, hb_in, g("w4_%d" % hh), g("wa1"), g("wa2_%d" % hh), g("ba_%d" % hh), g("ng"), IN["ident"], IN["mask"], IN["tris"],
                        o_d, ocol=hh * 512)
            else:
                phase_W(P, S, 4, hb_in, g("mixT"), g("vecsT_%d" % hh), g("wr_%d" % hh), g("wk_%d" % hh), g("wv_%d" % hh), g("rw1"), g("ra1"), g("rg1"),
                        g("w2_%d" % hh), g("a2_%d" % hh), g("g2_%d" % hh), g("vecs_%d" % hh), IN["rwcst"], o_d, ocol=hh * 512)
        if "Ra" in PHASE_SEL:
            phase_Ra(P, S, o_d, h_in, g("wo"), g("g1"), g("b1"), IN["ident"], h1_d, h1b_d)
        if "Rb" in PHASE_SEL:
            phase_Rb(P, S, h1_d, h1b_d, g("wup"), g("wdn"), g("g2"), g("b2"), IN["ident"], h2_d, h2b_d)
        h_out = out if n == len(layers) - 1 else hbuf[n % 2]
        hb_out = hbb[n % 2]
        if "Rc" in PHASE_SEL:
            phase_Rc(P, S, h2_d, h2b_d, IN["p%d" % i], g("wg"), g("wp"), IN["ident"], h_out, hb_out)
        h_in, hb_in = h_out, hb_out
    P.finish()
    return nc


LAYER_GROUPS = [[0, 1, 2, 3]]


def kernel(**inputs):
    inp = {k: np.asarray(v, dtype=np.float32) for k, v in inputs.items()}
    x = inp["x"]
    h = [np.ascontiguousarray(x[b]) for b in range(BATCH)]
    consts = const_inputs()
    for layers in LAYER_GROUPS:
        shared = dict(consts)
        for i in layers:
            shared.update(layer_inputs(inp, i))
        in_maps = []
        for c in range(8):
            b = c % BATCH
            m = dict(shared)
            m["x"] = h[b]
            for i in layers:
                m["p%d" % i] = np.ascontiguousarray(inp["p"][i, b])
            in_maps.append(m)
        shapes = {k: v.shape for k, v in in_maps[0].items()}
        nc = build_program(layers, shapes)
        res = run_bass_kernel_spmd(nc, in_maps, core_ids=list(range(8)))
        h = [np.asarray(res.results[b]["out"], dtype=np.float32) for b in range(BATCH)]
    return np.stack(h, 0).astype(np.float32)
```
